# Optimizing a Trainium2 kernel written in Bass

```python
import jax, jax.numpy as jnp
from jax import lax
import numpy as np

D_MODEL = 4096
BATCH = 2
SEQ = 4096
DEPTH = 2

CHUNK = 64
MIX_WIDTH = D_MODEL
GDN_HEAD_DIM = 128
GDN_WIDTH = MIX_WIDTH // 2
GDN_HEADS = GDN_WIDTH // GDN_HEAD_DIM
QKV_WIDTH = 3 * GDN_WIDTH
SHORT_CONV = 4
CONV_WIDTH = MIX_WIDTH - GDN_WIDTH
DW_KERNEL = 31
N_IN = QKV_WIDTH + GDN_WIDTH + 2 * GDN_HEADS + 2 * CONV_WIDTH
D_FF = ((8 * D_MODEL // 3 + 255) // 256) * 256
RMS_EPS = 1e-6
LN_EPS = 1e-5

kernel_name = "hybrid_gdn_conformer_conv_parallel"


def rmsnorm(x, w, eps=RMS_EPS):
    xf = x.astype(jnp.float32)
    y = xf * lax.rsqrt(jnp.mean(xf * xf, axis=-1, keepdims=True) + eps)
    return (y * w.astype(jnp.float32)).astype(x.dtype)


def layernorm(x, w, b, eps=LN_EPS):
    xf = x.astype(jnp.float32)
    mu = jnp.mean(xf, axis=-1, keepdims=True)
    var = jnp.mean(jnp.square(xf - mu), axis=-1, keepdims=True)
    y = (xf - mu) * lax.rsqrt(var + eps)
    return (y * w.astype(jnp.float32) + b.astype(jnp.float32)).astype(x.dtype)


def l2norm(x, eps=1e-6):
    return x * lax.rsqrt(jnp.sum(x * x, axis=-1, keepdims=True) + eps)


def causal_depthwise_conv(x, w):
    width, chans = w.shape
    return lax.conv_general_dilated(
        x, w[:, None, :].astype(x.dtype), window_strides=(1,), padding=[(width - 1, 0)],
        dimension_numbers=("NWC", "WIO", "NWC"), feature_group_count=chans)


def gated_delta_rule(q, k, v, g, beta):
    bsz, t_len, n_h, dk = q.shape
    dv = v.shape[-1]
    n_c = t_len // CHUNK
    q = l2norm(q.astype(jnp.float32)) * (dk ** -0.5)
    k = l2norm(k.astype(jnp.float32))
    v = v.astype(jnp.float32)

    def chunks(t):
        return t.reshape(bsz, n_c, CHUNK, n_h, -1).transpose(0, 3, 1, 2, 4)

    qc, kc, vc = chunks(q), chunks(k), chunks(v)
    gc = g.astype(jnp.float32).reshape(bsz, n_c, CHUNK, n_h).transpose(0, 3, 1, 2)
    bc = beta.astype(jnp.float32).reshape(bsz, n_c, CHUNK, n_h).transpose(0, 3, 1, 2)
    g_cum = jnp.cumsum(gc, axis=-1)

    causal = jnp.tril(jnp.ones((CHUNK, CHUNK), dtype=bool))
    strict = jnp.tril(jnp.ones((CHUNK, CHUNK), dtype=bool), k=-1)
    diff = g_cum[..., :, None] - g_cum[..., None, :]
    decay = jnp.exp(jnp.where(causal, diff, -jnp.inf))

    k_beta = kc * bc[..., None]
    lower = jnp.where(strict, jnp.einsum("bhnid,bhnjd->bhnij", k_beta, kc) * decay, 0.0)
    eye = jnp.eye(CHUNK, dtype=jnp.float32)
    t_inv = lax.linalg.triangular_solve(eye + lower, jnp.broadcast_to(eye, lower.shape),
                                        left_side=True, lower=True, unit_diagonal=True)
    u = jnp.einsum("bhnij,bhnjd->bhnid", t_inv, vc * bc[..., None])
    w = jnp.einsum("bhnij,bhnjd->bhnid", t_inv, k_beta * jnp.exp(g_cum)[..., None])
    a_intra = jnp.einsum("bhnid,bhnjd->bhnij", qc, kc) * decay

    def step(state, inp):
        q_i, k_i, u_i, w_i, g_i, a_i = inp
        v_new = u_i - jnp.einsum("bhcd,bhde->bhce", w_i, state)
        o_i = (jnp.einsum("bhcd,bhde->bhce", q_i * jnp.exp(g_i)[..., None], state)
               + jnp.einsum("bhij,bhje->bhie", a_i, v_new))
        g_last = g_i[..., -1]
        state = (state * jnp.exp(g_last)[..., None, None]
                 + jnp.einsum("bhcd,bhce->bhde", k_i * jnp.exp(g_last[..., None] - g_i)[..., None], v_new))
        return state, o_i

    mv = lambda t: jnp.moveaxis(t, 2, 0)
    s0 = jnp.zeros((bsz, n_h, dk, dv), jnp.float32)
    _, out = lax.scan(step, s0, (mv(qc), mv(kc), mv(u), mv(w), mv(g_cum), mv(a_intra)))
    return out.transpose(1, 0, 3, 2, 4).reshape(bsz, t_len, n_h, dv)


def hybrid_layer(x, pre_mix_norm, w_in, gdn_conv_w, gdn_a_log, gdn_dt_bias, gdn_norm_w,
                 cm_pw_b, cm_dw_w, cm_dw_b, cm_ln_w, cm_ln_b, w_out, post_mix_norm,
                 pre_ffn_norm, w_gate, w_up, w_down, post_ffn_norm):
    bsz, t_len, _ = x.shape
    h = rmsnorm(x, pre_mix_norm)
    proj = h @ w_in
    o1 = QKV_WIDTH
    o2 = o1 + GDN_WIDTH
    o3 = o2 + GDN_HEADS
    o4 = o3 + GDN_HEADS
    qkv, z, b_logit, a_logit, glu_in = jnp.split(proj, [o1, o2, o3, o4], axis=-1)

    qkv = jax.nn.silu(causal_depthwise_conv(qkv, gdn_conv_w))
    q, k, v = jnp.split(qkv, 3, axis=-1)
    heads = lambda t: t.reshape(bsz, t_len, GDN_HEADS, GDN_HEAD_DIM)
    beta = jax.nn.sigmoid(b_logit.astype(jnp.float32))
    g = -jnp.exp(gdn_a_log.astype(jnp.float32)) * jax.nn.softplus(
        a_logit.astype(jnp.float32) + gdn_dt_bias.astype(jnp.float32))
    o_a = gated_delta_rule(heads(q), heads(k), heads(v), g, beta)
    o_a = rmsnorm(o_a, gdn_norm_w) * jax.nn.silu(heads(z).astype(jnp.float32))
    o_a = o_a.reshape(bsz, t_len, GDN_WIDTH).astype(x.dtype)

    c_val, c_gate = jnp.split(glu_in + cm_pw_b, 2, axis=-1)
    c = c_val * jax.nn.sigmoid(c_gate)
    c = causal_depthwise_conv(c, cm_dw_w) + cm_dw_b
    c = jax.nn.silu(layernorm(c, cm_ln_w, cm_ln_b))

    mix = jnp.concatenate([o_a, c], axis=-1) @ w_out
    x = x + rmsnorm(mix, post_mix_norm)

    hf = rmsnorm(x, pre_ffn_norm)
    ff = (jax.nn.silu(hf @ w_gate) * (hf @ w_up)) @ w_down
    return x + rmsnorm(ff, post_ffn_norm)


def setup_inputs(seed: int = 0) -> dict:
    key = jax.random.key(seed)
    ks = jax.random.split(key, 20)
    L = DEPTH
    nrm = lambda k, shape, s: jax.random.normal(k, shape, jnp.float32) * s
    return {
        "x": nrm(ks[0], (BATCH, SEQ, D_MODEL), 1.0),
        "pre_mix_norm": 1.0 + nrm(ks[1], (L, D_MODEL), 0.02),
        "w_in": nrm(ks[2], (L, D_MODEL, N_IN), D_MODEL ** -0.5),
        "gdn_conv_w": nrm(ks[3], (L, SHORT_CONV, QKV_WIDTH), SHORT_CONV ** -0.5),
        "gdn_a_log": jnp.log(jax.random.uniform(ks[4], (L, GDN_HEADS), jnp.float32, 1.0, 16.0)),
        "gdn_dt_bias": nrm(ks[5], (L, GDN_HEADS), 0.1),
        "gdn_norm_w": 1.0 + nrm(ks[6], (L, GDN_HEAD_DIM), 0.02),
        "cm_pw_b": nrm(ks[7], (L, 2 * CONV_WIDTH), 0.02),
        "cm_dw_w": nrm(ks[8], (L, DW_KERNEL, CONV_WIDTH), DW_KERNEL ** -0.5),
        "cm_dw_b": nrm(ks[9], (L, CONV_WIDTH), 0.02),
        "cm_ln_w": 1.0 + nrm(ks[10], (L, CONV_WIDTH), 0.02),
        "cm_ln_b": nrm(ks[11], (L, CONV_WIDTH), 0.02),
        "w_out": nrm(ks[12], (L, MIX_WIDTH, D_MODEL), MIX_WIDTH ** -0.5),
        "post_mix_norm": 1.0 + nrm(ks[13], (L, D_MODEL), 0.02),
        "pre_ffn_norm": 1.0 + nrm(ks[14], (L, D_MODEL), 0.02),
        "w_gate": nrm(ks[15], (L, D_MODEL, D_FF), D_MODEL ** -0.5),
        "w_up": nrm(ks[16], (L, D_MODEL, D_FF), D_MODEL ** -0.5),
        "w_down": nrm(ks[17], (L, D_FF, D_MODEL), D_FF ** -0.5),
        "post_ffn_norm": 1.0 + nrm(ks[18], (L, D_MODEL), 0.02),
    }


def reference(x, pre_mix_norm, w_in, gdn_conv_w, gdn_a_log, gdn_dt_bias, gdn_norm_w,
              cm_pw_b, cm_dw_w, cm_dw_b, cm_ln_w, cm_ln_b, w_out, post_mix_norm,
              pre_ffn_norm, w_gate, w_up, w_down, post_ffn_norm):
    for l in range(DEPTH):
        x = hybrid_layer(x, pre_mix_norm[l], w_in[l], gdn_conv_w[l], gdn_a_log[l], gdn_dt_bias[l],
                         gdn_norm_w[l], cm_pw_b[l], cm_dw_w[l], cm_dw_b[l], cm_ln_w[l], cm_ln_b[l],
                         w_out[l], post_mix_norm[l], pre_ffn_norm[l], w_gate[l], w_up[l], w_down[l],
                         post_ffn_norm[l])
    return x
```

```python
import os
import numpy as np
import concourse.bass as bass
import concourse.mybir as mybir

F32 = mybir.dt.float32
BF16 = mybir.dt.bfloat16
AF = mybir.ActivationFunctionType
ALU = mybir.AluOpType
AX = mybir.AxisListType

ENGS = ("pe", "act", "dve", "pool", "sp")
SEM_EPOCH = 1 << 30


class A:
    def __init__(self, *a, **k):
        self.a = a
        self.k = k


class T:
    __slots__ = ("ap", "name", "w", "wh", "rh", "excl")

    def __init__(self, ap, name="", excl=False):
        self.ap = ap
        self.name = name
        self.excl = excl
        self.w = None
        self.wh = {}
        self.rh = {}

    def __getitem__(self, idx):
        return V(self, self.ap[idx])


class V:
    __slots__ = ("t", "ap")

    def __init__(self, t, ap):
        self.t = t
        self.ap = ap

    def __getitem__(self, idx):
        return V(self.t, self.ap[idx])


def _tile(x):
    return x.t if isinstance(x, V) else x


def _ap(x):
    return x.ap if isinstance(x, (V,)) else (x.ap if isinstance(x, T) else x)


class Prog:
    def __init__(self, nc, same_engine_sync=True, n_dma_sems=24):
        self.nc = nc
        self.es = nc
        self.lists = {e: [] for e in ENGS}
        self.cnt = {e: 0 for e in ENGS}
        self.epoch = {e: 0 for e in ENGS}
        self.sems = {e: [] for e in ENGS}
        self.waited = {e: {} for e in ENGS}
        self.same = same_engine_sync
        self.ctx = []
        self._sem_ctx = []
        self.dma_sems = {}
        self.dma_rr = {}
        self.n_dma_sems = n_dma_sems
        self.n_inst = 0

    def _enter(self, cm):
        v = cm.__enter__()
        self.ctx.append(cm)
        return v

    def close(self):
        for cm in reversed(self.ctx):
            cm.__exit__(None, None, None)
        self.ctx = []
        for cm in reversed(self._sem_ctx):
            cm.__exit__(None, None, None)
        self._sem_ctx = []

    def mark(self):
        return len(self.ctx)

    def release(self, mark):
        while len(self.ctx) > mark:
            self.ctx.pop().__exit__(None, None, None)

    def barrier(self):
        for e in ENGS:
            waits = {}
            for f in ENGS:
                if f != e and self.cnt[f] > 0:
                    self._need(e, ("eng", f, self.epoch[f], self.cnt[f]), waits)
            for q, pool in self.dma_sems.items():
                for slot in pool:
                    if slot[1] > 0:
                        self._need(e, ("dma", (q, slot[2]), slot[0], slot[1]), waits)
            self._emit_waits(e, waits)

    def sem(self, name):
        cm = self.nc.semaphore(name)
        v = cm.__enter__()
        self._sem_ctx.append(cm)
        return v

    def sbuf(self, name, shape, dtype):
        t = self._enter(self.nc.sbuf_tensor("s_" + name, list(shape), dtype))
        return T(t[:], name)

    def psum(self, name, shape, dtype):
        t = self._enter(self.nc.psum_tensor("p_" + name, list(shape), dtype))
        return T(t[:], name, excl=True)

    def dram(self, ap, name=""):
        return T(ap, name)

    def _eng_sem(self, e):
        ep = self.epoch[e]
        while len(self.sems[e]) <= ep:
            self.sems[e].append(self.sem(f"p_{e}_{len(self.sems[e])}"))
        return self.sems[e][ep]

    def _need(self, e, tok, waits):
        if tok is None:
            return
        if tok[0] == "eng":
            _, pe_, ep, c = tok
            if pe_ == e and (not self.same or e == "pe"):
                return
            key = ("eng", pe_, ep)
            sem = self.sems[pe_][ep]
        else:
            _, sid, sem, c = tok
            key = ("dma", sid)
        if self.waited[e].get(key, 0) >= c:
            return
        prev = waits.get(key)
        if prev is None or prev[1] < c:
            waits[key] = (sem, c)

    def _collect(self, e, reads, writes):
        waits = {}
        for r in reads:
            t = _tile(r)
            for tok in t.wh.values():
                self._need(e, tok, waits)
            if t.excl:
                for tok in t.rh.values():
                    self._need(e, tok, waits)
        for w in writes:
            t = _tile(w)
            for tok in t.wh.values():
                self._need(e, tok, waits)
            for tok in t.rh.values():
                self._need(e, tok, waits)
        return waits

    def _emit_waits(self, e, waits):
        for key, (sem, c) in waits.items():
            self.waited[e][key] = c
            self.lists[e].append(("wait", sem, c))

    @staticmethod
    def _key(tok):
        return ("eng", tok[1], tok[2]) if tok[0] == "eng" else ("dma", tok[1])

    def _mark(self, tok, reads, writes):
        k = self._key(tok)
        for r in reads:
            _tile(r).rh[k] = tok
        for w in writes:
            t = _tile(w)
            t.w = tok
            t.wh[k] = tok

    def op(self, e, meth, args=(), reads=(), writes=(), **kw):
        if callable(meth):
            fn = meth
        else:
            fn = (lambda eng, meth=meth, a=args.a, k=args.k: getattr(eng, meth)(*a, **k))
        waits = self._collect(e, reads, writes)
        self._emit_waits(e, waits)
        if self.cnt[e] >= SEM_EPOCH:
            self.epoch[e] += 1
            self.cnt[e] = 0
        sem = self._eng_sem(e)
        self.cnt[e] += 1
        tok = ("eng", e, self.epoch[e], self.cnt[e])
        self.lists[e].append(("inst", fn, sem, 1))
        self._mark(tok, reads, writes)
        self.n_inst += 1
        return tok

    def dma(self, q, out, in_, **kw):
        waits = self._collect(q, [in_], [out])
        pool = self.dma_sems.setdefault(q, [])
        if len(pool) < self.n_dma_sems:
            pool.append([self.sem(f"d_{q}_{len(pool)}"), 0, len(pool)])
            slot = pool[-1]
        else:
            i = self.dma_rr.get(q, 0)
            self.dma_rr[q] = (i + 1) % len(pool)
            slot = pool[i]
            if slot[1] > 0:
                self._need(q, ("dma", (q, slot[2]), slot[0], slot[1]), waits)
        self._emit_waits(q, waits)
        slot[1] += 16
        tok = ("dma", (q, slot[2]), slot[0], slot[1])
        o_ap, i_ap = _ap(out), _ap(in_)
        self.lists[q].append(("inst", lambda eng: eng.dma_start(out=o_ap, in_=i_ap, **kw), slot[0], 16))
        self._mark(tok, [in_], [out])
        return tok

    def dma_like(self, q, fn, reads, writes):
        waits = self._collect(q, reads, writes)
        pool = self.dma_sems.setdefault(q, [])
        if len(pool) < self.n_dma_sems:
            pool.append([self.sem(f"d_{q}_{len(pool)}"), 0, len(pool)])
            slot = pool[-1]
        else:
            i = self.dma_rr.get(q, 0)
            self.dma_rr[q] = (i + 1) % len(pool)
            slot = pool[i]
            if slot[1] > 0:
                self._need(q, ("dma", (q, slot[2]), slot[0], slot[1]), waits)
        self._emit_waits(q, waits)
        slot[1] += 16
        tok = ("dma", (q, slot[2]), slot[0], slot[1])
        self.lists[q].append(("inst", fn, slot[0], 16))
        self._mark(tok, reads, writes)
        return tok

    def wait_tile(self, e, tiles):
        waits = {}
        for t in tiles:
            self._need(e, _tile(t).w, waits)
        self._emit_waits(e, waits)

    def emit(self):
        nc = self.nc
        lists = self.lists

        def run(eng, items):
            for it in items:
                if it[0] == "wait":
                    eng.wait_ge(it[1], it[2])
                else:
                    ins = it[1](eng)
                    ins.then_inc(it[2], it[3])

        with nc.Block() as block:
            @block.tensor
            def _(eng):
                run(eng, lists["pe"])

            @block.scalar
            def _(eng):
                run(eng, lists["act"])

            @block.vector
            def _(eng):
                run(eng, lists["dve"])

            @block.gpsimd
            def _(eng):
                run(eng, lists["pool"])

            @block.sync
            def _(eng):
                run(eng, lists["sp"])


D = int(os.environ.get("MK_D", 4096))
DFF = int(os.environ.get("MK_DFF", 11008))
KC = D // 128
FC = DFF // 128
HALF = FC // 2
EPS = 1e-6


class Ctx:
    pass


def setup_common(P, TB):
    c = Ctx()
    c.P = P
    c.TB = TB
    c.ones = P.sbuf("ones", [128, 128], BF16)
    P.op("dve", "memset", A(c.ones.ap, 1.0), writes=[c.ones])
    c.NW = 4
    c.wbig = P.sbuf("wbuf", [128, c.NW, max(HALF, KC) * 128], BF16)
    c.wb = [T(c.wbig.ap[:, i, :], f"wb{i}") for i in range(c.NW)]
    c.wi = 0
    c.ps = [P.psum(f"ps{i}", [128, 512], F32) for i in range(7)]
    c.pi = 0
    c.ss = P.psum("ss", [128, 512], F32)
    c.sq = [P.sbuf(f"sq{i}", [128, 512], BF16) for i in range(2)]
    c.sqi = 0
    c.rstd = P.sbuf("rstd", [128, 512], F32)
    return c


def load_w(c, W, k0, nkc, m0, mw=128):
    P = c.P
    buf = c.wb[c.wi % c.NW]
    c.wi += 1
    dst = V(buf, buf.ap[:, 0:nkc * mw].rearrange("p (k m) -> p k m", m=mw))
    if getattr(c, "tiled", False):
        blk = (m0 // 128) if k0 == 0 else (KC + m0 // 128)
        P.dma("pool", V(buf, buf.ap[:, 0:nkc * mw]), T(W[blk]))
        return dst
    src = T(W[k0 * 128:(k0 + nkc) * 128, m0:m0 + mw].rearrange("(k p) m -> p k m", p=128))
    P.dma("pool", dst, src)
    return dst


def next_ps(c):
    t = c.ps[c.pi % len(c.ps)]
    c.pi += 1
    return t


def mm_acc(c, ps, wv, acts, n, first=True, last=True):
    P = c.P
    nk = len(acts)
    for k in range(nk):
        a = acts[k]
        P.op("pe", "matmul", A(ps.ap[:, 0:n], wv.ap[:, k, :], _ap(a), start=(first and k == 0), stop=(last and k == nk - 1)),
             reads=[wv, a], writes=[ps])


def sumsq_step(c, src, n, first, last):
    P = c.P
    sq = c.sq[c.sqi % 2]
    c.sqi += 1
    P.op("act", "activation", A(sq.ap[:, 0:n], _ap(src), AF.Square), reads=[src], writes=[sq])
    P.op("pe", "matmul", A(c.ss.ap[:, 0:n], c.ones.ap, sq.ap[:, 0:n], start=first, stop=last), reads=[c.ones, sq], writes=[c.ss])


def finish_rstd(c, n, dim):
    P = c.P
    P.op("dve", "tensor_scalar", A(c.rstd.ap[:, 0:n], c.ss.ap[:, 0:n], 1.0 / dim, EPS, ALU.mult, ALU.add), reads=[c.ss], writes=[c.rstd])
    P.op("act", "activation", A(c.rstd.ap[:, 0:n], c.rstd.ap[:, 0:n], AF.Sqrt), reads=[c.rstd], writes=[c.rstd])
    P.op("dve", "reciprocal", A(c.rstd.ap[:, 0:n], c.rstd.ap[:, 0:n]), reads=[c.rstd], writes=[c.rstd])


def build_p3(nc, NTOK, TB=512, want_h=True):
    P = Prog(nc)
    dt = nc.dram_tensor
    mixT = dt("mixT", [D, NTOK], BF16, kind="ExternalInput").ap()
    xT = dt("xT", [D, NTOK], F32, kind="ExternalInput").ap()
    TILED = os.environ.get("MK_WTILED", "0") == "1"
    if TILED:
        w_out = dt("w_out", [KC, 128, KC * 128], BF16, kind="ExternalInput").ap()
        w_gate = dt("w_gate", [FC, 128, KC * 128], BF16, kind="ExternalInput").ap()
        w_up = dt("w_up", [FC, 128, KC * 128], BF16, kind="ExternalInput").ap()
        w_down = dt("w_down", [2 * KC, 128, HALF * 128], BF16, kind="ExternalInput").ap()
    else:
        w_out = dt("w_out", [D, D], F32, kind="ExternalInput").ap()
        w_gate = dt("w_gate", [D, DFF], F32, kind="ExternalInput").ap()
        w_up = dt("w_up", [D, DFF], F32, kind="ExternalInput").ap()
        w_down = dt("w_down", [DFF, D], F32, kind="ExternalInput").ap()
    c_tiled = TILED
    norms = dt("norms", [128, 4, KC], F32, kind="ExternalInput").ap()
    x_out = dt("x_out", [D, NTOK], F32, kind="ExternalOutput").ap()
    h_out = dt("h_out", [D, NTOK], BF16, kind="ExternalOutput").ap()
    x1d = dt("x1_scratch", [D, NTOK], F32, kind="ExternalOutput").ap()

    c = setup_common(P, TB)
    c.tiled = c_tiled
    nrm = P.sbuf("nrm", [128, 4, KC], F32)
    P.dma("sp", nrm, T(norms))
    mh_big = P.sbuf("mh", [128, KC, TB], BF16)
    mh = [T(mh_big.ap[:, k, :], f"mh{k}") for k in range(KC)]
    y_big = P.sbuf("y", [128, KC, TB], F32)
    y = [T(y_big.ap[:, k, :], f"y{k}") for k in range(KC)]
    act_big = P.sbuf("act", [128, HALF, TB], BF16)
    act = [T(act_big.ap[:, k, :], f"act{k}") for k in range(HALF)]
    stg = [P.sbuf(f"stg{i}", [128, TB], F32) for i in range(2)]
    sg = [P.sbuf(f"sg{i}", [128, TB], F32) for i in range(2)]
    hstg = [P.sbuf(f"hstg{i}", [128, TB], BF16) for i in range(2)]
    x1T = T(x1d)
    outs = []

    def feat(ap, m, t0):
        return ap[m * 128:(m + 1) * 128, t0:t0 + TB]

    for tb in range(NTOK // TB):
        t0 = tb * TB
        for k in range(KC):
            P.dma("sp", mh[k], T(feat(mixT, k, t0)))
        for m in range(KC):
            wv = load_w(c, w_out, 0, KC, m * 128)
            ps = next_ps(c)
            mm_acc(c, ps, wv, mh, TB)
            P.op("act", "copy", A(y[m].ap, ps.ap), reads=[ps], writes=[y[m]])
            sumsq_step(c, ps, TB, m == 0, m == KC - 1)
        if os.environ.get("MK_STOP") == "b":
            for m in range(KC):
                o = T(feat(x_out, m, t0)); outs.append(o)
                P.dma("sp", o, y[m])
            continue
        finish_rstd(c, TB, D)
        if os.environ.get("MK_STOP") == "c":
            for m in range(KC):
                o = T(feat(x_out, m, t0)); outs.append(o)
                P.dma("sp", o, y[m])
            o = T(x_out[0:128, 0:TB]); outs.append(o)
            P.dma("sp", o, c.rstd)
            continue
        P.dma("sp", stg[0], T(feat(xT, 0, t0)))
        for m in range(KC):
            st = stg[m % 2]
            if m + 1 < KC:
                P.dma("sp", stg[(m + 1) % 2], T(feat(xT, m + 1, t0)))
            P.op("dve", "scalar_tensor_tensor", A(y[m].ap, y[m].ap, nrm.ap[:, 0, m:m + 1], c.rstd.ap, ALU.mult, ALU.mult),
                 reads=[y[m], nrm, c.rstd], writes=[y[m]])
            P.op("dve", "tensor_tensor", A(y[m].ap, y[m].ap, st.ap, ALU.add), reads=[y[m], st], writes=[y[m]])
            P.dma("sp", V(x1T, feat(x1d, m, t0)), y[m])
            sumsq_step(c, y[m], TB, m == 0, m == KC - 1)
        finish_rstd(c, TB, D)
        for m in range(KC):
            P.op("dve", "scalar_tensor_tensor", A(mh[m].ap, y[m].ap, nrm.ap[:, 1, m:m + 1], c.rstd.ap, ALU.mult, ALU.mult),
                 reads=[y[m], nrm, c.rstd], writes=[mh[m]])
        for half in range(2):
            for j in range(HALF):
                col = (half * HALF + j) * 128
                wg = load_w(c, w_gate, 0, KC, col)
                psg = next_ps(c)
                mm_acc(c, psg, wg, mh, TB)
                wu = load_w(c, w_up, 0, KC, col)
                psu = next_ps(c)
                mm_acc(c, psu, wu, mh, TB)
                s = sg[j % 2]
                P.op("act", "activation", A(s.ap, psg.ap, AF.Silu), reads=[psg], writes=[s])
                P.op("dve", "tensor_tensor", A(act[j].ap, s.ap, psu.ap, ALU.mult), reads=[s, psu], writes=[act[j]])
            for m in range(KC):
                wv = load_w(c, w_down, half * HALF, HALF, m * 128)
                ps = next_ps(c)
                mm_acc(c, ps, wv, act, TB)
                if half == 0:
                    P.op("act", "copy", A(y[m].ap, ps.ap), reads=[ps], writes=[y[m]])
                else:
                    P.op("dve", "tensor_tensor", A(y[m].ap, y[m].ap, ps.ap, ALU.add), reads=[ps, y[m]], writes=[y[m]])
                    sumsq_step(c, y[m], TB, m == 0, m == KC - 1)
        finish_rstd(c, TB, D)
        P.dma("sp", stg[0], V(x1T, feat(x1d, 0, t0)))
        for m in range(KC):
            st = stg[m % 2]
            if m + 1 < KC:
                P.dma("sp", stg[(m + 1) % 2], V(x1T, feat(x1d, m + 1, t0)))
            P.op("dve", "scalar_tensor_tensor", A(y[m].ap, y[m].ap, nrm.ap[:, 2, m:m + 1], c.rstd.ap, ALU.mult, ALU.mult),
                 reads=[y[m], nrm, c.rstd], writes=[y[m]])
            P.op("dve", "tensor_tensor", A(y[m].ap, y[m].ap, st.ap, ALU.add), reads=[y[m], st], writes=[y[m]])
            o = T(feat(x_out, m, t0))
            outs.append(o)
            P.dma("sp", o, y[m])
            if want_h:
                sumsq_step(c, y[m], TB, m == 0, m == KC - 1)
        if want_h:
            finish_rstd(c, TB, D)
            for m in range(KC):
                hs = hstg[m % 2]
                P.op("dve", "scalar_tensor_tensor", A(hs.ap, y[m].ap, nrm.ap[:, 3, m:m + 1], c.rstd.ap, ALU.mult, ALU.mult),
                     reads=[y[m], nrm, c.rstd], writes=[hs])
                o = T(feat(h_out, m, t0))
                outs.append(o)
                P.dma("sp", o, hs)
    P.wait_tile("sp", outs)
    P.emit()
    P.close()
    return P


def build_p1(nc, NTOK, TB=512):
    P = Prog(nc)
    dt = nc.dram_tensor
    xT = dt("xT", [D, NTOK], F32, kind="ExternalInput").ap()
    norms = dt("norms", [128, 1, KC], F32, kind="ExternalInput").ap()
    h_out = dt("h_out", [D, NTOK], BF16, kind="ExternalOutput").ap()
    c = Ctx()
    c.P = P
    c.ones = P.sbuf("ones", [128, 128], BF16)
    P.op("dve", "memset", A(c.ones.ap, 1.0), writes=[c.ones])
    c.ss = P.psum("ss", [128, 512], F32)
    c.sq = [P.sbuf(f"sq{i}", [128, 512], BF16) for i in range(2)]
    c.sqi = 0
    c.rstd = P.sbuf("rstd", [128, 512], F32)
    nrm = P.sbuf("nrm", [128, 1, KC], F32)
    P.dma("sp", nrm, T(norms))
    y_big = P.sbuf("y", [128, KC, TB], F32)
    y = [T(y_big.ap[:, k, :], f"y{k}") for k in range(KC)]
    hstg = [P.sbuf(f"hstg{i}", [128, TB], BF16) for i in range(2)]
    outs = []
    for tb in range(NTOK // TB):
        t0 = tb * TB
        for m in range(KC):
            P.dma("sp", y[m], T(xT[m * 128:(m + 1) * 128, t0:t0 + TB]))
            sumsq_step(c, y[m], TB, m == 0, m == KC - 1)
        finish_rstd(c, TB, D)
        for m in range(KC):
            hs = hstg[m % 2]
            P.op("dve", "scalar_tensor_tensor", A(hs.ap, y[m].ap, nrm.ap[:, 0, m:m + 1], c.rstd.ap, ALU.mult, ALU.mult),
                 reads=[y[m], nrm, c.rstd], writes=[hs])
            o = T(h_out[m * 128:(m + 1) * 128, t0:t0 + TB])
            outs.append(o)
            P.dma("sp", o, hs)
    P.wait_tile("sp", outs)
    P.emit()
    P.close()
    return P


D = int(os.environ.get("MK_D", 4096))
KC = D // 128
HD = 128
C = 128
NH = 4
NCH = 16
DWK = 31
RMS_EPS = 1e-6
LN_EPS = 1e-5


def build_p2(nc, NTOK, TBK=512, NCV=1024, do_gdn=True, do_cv=True):
    P = Prog(nc)
    dt = nc.dram_tensor
    hT = dt("hT", [D, NTOK], BF16, kind="ExternalInput").ap()
    hT_cv = dt("hT_cv", [D, 32 + NCV], BF16, kind="ExternalInput").ap()
    w_gdn = dt("w_gdn", [D, 16 * 128 + 8], F32, kind="ExternalInput").ap()
    w_cv = dt("w_cv", [D, 4096], F32, kind="ExternalInput").ap()
    convw = dt("convw", [128, 12, 4], F32, kind="ExternalInput").ap()
    hp = dt("hp", [128, 2], F32, kind="ExternalInput").ap()
    gnw = dt("gnw", [128, 1], F32, kind="ExternalInput").ap()
    cvp = dt("cvp", [128, 32 + 16 * 3], F32, kind="ExternalInput").ap()
    dww = dt("dww", [128, NCH, DWK], F32, kind="ExternalInput").ap()
    halo_mask = dt("halo_mask", [128, 1], F32, kind="ExternalInput").ap()
    ident_f_d = dt("ident_f", [128, 128], F32, kind="ExternalInput").ap()
    masks_d = dt("masks", [128, 2, 128], F32, kind="ExternalInput").ap()
    sel_d = dt("sel", [128, 4, 128], F32, kind="ExternalInput").ap()
    oaT = dt("oaT", [NH * HD, NTOK], BF16, kind="ExternalOutput").ap()
    cT = dt("cT", [NCH * 128, NCV], BF16, kind="ExternalOutput").ap()
    outs = []

    ident_f = P.sbuf("ident_f", [128, 128], F32)
    P.dma("sp", ident_f, T(ident_f_d))
    ident_b = P.sbuf("ident_b", [128, 128], BF16)
    P.op("dve", "tensor_copy", A(ident_b.ap, ident_f.ap), reads=[ident_f], writes=[ident_b])
    ones_b = P.sbuf("ones_b", [128, 128], BF16)
    P.op("dve", "memset", A(ones_b.ap, 1.0), writes=[ones_b])
    NW = int(os.environ.get("MK_NW", 4))
    wbig = P.sbuf("wbuf", [128, NW, KC * 128], BF16)
    wb = [T(wbig.ap[:, i, :], f"wb{i}") for i in range(NW)]
    wi = [0]

    def load_w(W, m0, mw=128):
        buf = wb[wi[0] % NW]
        wi[0] += 1
        dst = V(buf, buf.ap[:, 0:KC * mw].rearrange("p (k m) -> p k m", m=mw))
        src = T(W[:, m0:m0 + mw].rearrange("(k p) m -> p k m", p=128))
        P.dma("pool", dst, src)
        return dst

    bankF = [P.psum(f"bankF{i}", [128, 512], F32) for i in range(6)]
    bankB = [P.psum(f"bankB{i}", [128, 1024], BF16) for i in range(2)]

    def recip_sqrt(t_ap, tile):
        P.op("act", "activation", A(t_ap, t_ap, AF.Sqrt), reads=[tile], writes=[tile])
        P.op("dve", "reciprocal", A(t_ap, t_ap), reads=[tile], writes=[tile])

    if do_cv:
        NTC = 32 + NCV
        NBC = NCV // 512
        cvps = P.sbuf("cvp", [128, 80], F32)
        P.dma("sp", cvps, T(cvp))
        dwws = P.sbuf("dww", [128, NCH, DWK], F32)
        P.dma("sp", dwws, T(dww))
        hmask = P.sbuf("hmask", [128, 1], F32)
        P.dma("sp", hmask, T(halo_mask))
        mcin = P.mark()
        cin = [P.sbuf(f"cin{i}", [128, NTC], BF16) for i in range(NCH)]
        m0 = P.mark()
        hcv_big = P.sbuf("hcv", [128, KC, NTC], BF16)
        hcv = [T(hcv_big.ap[:, k, :], f"hcv{k}") for k in range(KC)]
        sgt = [P.sbuf(f"sgt{i}", [128, NTC], F32) for i in range(2)]
        for k in range(KC):
            P.dma("sp", hcv[k], T(hT_cv[k * 128:(k + 1) * 128, :]))
        pieces = [(0, 512), (512, 1024), (1024, NTC)]
        for p in range(NCH):
            wg = load_w(w_cv, 2048 + p * 128)
            for bi, (a, b) in enumerate(pieces):
                for k in range(KC):
                    P.op("pe", "matmul", A(bankF[bi].ap[:, 0:b - a], wg.ap[:, k, :], hcv[k].ap[:, a:b], start=(k == 0), stop=(k == KC - 1)),
                         reads=[wg, hcv[k]], writes=[bankF[bi]])
            wv = load_w(w_cv, p * 128)
            for bi, (a, b) in enumerate(pieces):
                for k in range(KC):
                    P.op("pe", "matmul", A(bankF[3 + bi].ap[:, 0:b - a], wv.ap[:, k, :], hcv[k].ap[:, a:b], start=(k == 0), stop=(k == KC - 1)),
                         reads=[wv, hcv[k]], writes=[bankF[3 + bi]])
            sg = sgt[p % 2]
            for bi, (a, b) in enumerate(pieces):
                P.op("act", "activation", A(sg.ap[:, a:b], bankF[bi].ap[:, 0:b - a], AF.Sigmoid, bias=cvps.ap[:, 16 + p:17 + p]), reads=[bankF[bi], cvps], writes=[sg])
            for bi, (a, b) in enumerate(pieces):
                P.op("dve", "scalar_tensor_tensor", A(cin[p].ap[:, a:b], bankF[3 + bi].ap[:, 0:b - a], cvps.ap[:, p:p + 1], sg.ap[:, a:b], ALU.add, ALU.mult),
                     reads=[bankF[3 + bi], cvps, sg], writes=[cin[p]])
            P.op("dve", "tensor_scalar", A(cin[p].ap[:, 0:32], cin[p].ap[:, 0:32], hmask.ap[:, 0:1], None, ALU.mult), reads=[cin[p], hmask], writes=[cin[p]])
        P.barrier()
        P.release(m0)
        cconv = [P.sbuf(f"cconv{i}", [128, NCV], F32) for i in range(NCH)]
        ddg = [P.sbuf(f"ddg{i}", [128, DWK, 128], BF16) for i in range(2)]
        xb = [P.sbuf(f"xb{i}", [128, 512], BF16) for i in range(2)]
        sq2 = [P.sbuf(f"sq2{i}", [128, 512], BF16) for i in range(2)]
        mean = P.sbuf("mean", [128, NCV], F32)
        rstd = P.sbuf("rstdc", [128, NCV], F32)
        cst = [P.sbuf(f"cst{i}", [128, NCV], BF16) for i in range(2)]
        tmpc = [P.sbuf(f"tmpc{i}", [128, NCV], F32) for i in range(2)]
        cnt = 0
        for ch in range(NCH):
            dd = ddg[ch % 2]
            for k in range(DWK):
                P.op("dve", "tensor_scalar", A(dd.ap[:, k, :], ident_f.ap, dwws.ap[:, ch, k:k + 1], None, ALU.mult), reads=[ident_f, dwws], writes=[dd])
            for nb in range(NBC):
                ps = bankF[nb % 2]
                for k in range(DWK):
                    P.op("pe", "matmul", A(ps.ap, dd.ap[:, k, :], cin[ch].ap[:, 2 + k + nb * 512:2 + k + (nb + 1) * 512], start=(k == 0), stop=(k == DWK - 1)),
                         reads=[dd, cin[ch]], writes=[ps])
                sl = slice(nb * 512, (nb + 1) * 512)
                P.op("act", "activation", A(cconv[ch].ap[:, sl], ps.ap, AF.Identity, bias=cvps.ap[:, 32 + ch:33 + ch]), reads=[ps, cvps], writes=[cconv[ch]])
                x_, s_ = xb[cnt % 2], sq2[cnt % 2]
                cnt += 1
                P.op("act", "copy", A(x_.ap, cconv[ch].ap[:, sl]), reads=[cconv[ch]], writes=[x_])
                P.op("act", "activation", A(s_.ap, cconv[ch].ap[:, sl], AF.Square), reads=[cconv[ch]], writes=[s_])
                P.op("pe", "matmul", A(bankF[2 + nb].ap, ones_b.ap, x_.ap, start=(ch == 0), stop=(ch == NCH - 1)), reads=[ones_b, x_], writes=[bankF[2 + nb]])
                P.op("pe", "matmul", A(bankF[4 + nb].ap, ones_b.ap, s_.ap, start=(ch == 0), stop=(ch == NCH - 1)), reads=[ones_b, s_], writes=[bankF[4 + nb]])
        for nb in range(NBC):
            sl = slice(nb * 512, (nb + 1) * 512)
            P.op("dve", "tensor_scalar", A(mean.ap[:, sl], bankF[2 + nb].ap, 1.0 / (NCH * 128), None, ALU.mult), reads=[bankF[2 + nb]], writes=[mean])
            P.op("dve", "tensor_scalar", A(rstd.ap[:, sl], bankF[4 + nb].ap, 1.0 / (NCH * 128), LN_EPS, ALU.mult, ALU.add), reads=[bankF[4 + nb]], writes=[rstd])
        P.op("dve", "tensor_tensor", A(tmpc[0].ap, mean.ap, mean.ap, ALU.mult), reads=[mean], writes=[tmpc[0]])
        P.op("dve", "tensor_tensor", A(rstd.ap, rstd.ap, tmpc[0].ap, ALU.subtract), reads=[rstd, tmpc[0]], writes=[rstd])
        recip_sqrt(rstd.ap, rstd)
        for ch in range(NCH):
            tt = tmpc[ch % 2]
            P.op("dve", "tensor_tensor", A(tt.ap, cconv[ch].ap, mean.ap, ALU.subtract), reads=[cconv[ch], mean], writes=[tt])
            P.op("dve", "tensor_tensor", A(tt.ap, tt.ap, rstd.ap, ALU.mult), reads=[tt, rstd], writes=[tt])
            P.op("act", "activation", A(cst[ch % 2].ap, tt.ap, AF.Silu, bias=cvps.ap[:, 64 + ch:65 + ch], scale=cvps.ap[:, 48 + ch:49 + ch]), reads=[tt, cvps], writes=[cst[ch % 2]])
            o = T(cT[ch * 128:(ch + 1) * 128, :])
            outs.append(o)
            P.dma("sp", o, cst[ch % 2])
        P.barrier()
        P.release(mcin)


    if do_gdn:
        NB = TBK // 512
        NCK = TBK // C
        masks = P.sbuf("masks", [128, 2, 128], F32)
        P.dma("sp", masks, T(masks_d))
        sel = P.sbuf("sel", [128, 4, 128], F32)
        P.dma("sp", sel, T(sel_d))
        cw = P.sbuf("convw", [128, 12, 4], F32)
        P.dma("sp", cw, T(convw))
        hps = P.sbuf("hp", [128, 2], F32)
        P.dma("sp", hps, T(hp))
        gn = P.sbuf("gnw", [128, 1], F32)
        P.dma("sp", gn, T(gnw))
        NR = 68
        nA = P.sbuf("nA", [128, 1], F32)
        P.op("act", "activation", A(nA.ap, hps.ap[:, 0:1], AF.Exp), reads=[hps], writes=[nA])
        P.op("dve", "tensor_scalar", A(nA.ap, nA.ap, -1.0, None, ALU.mult), reads=[nA], writes=[nA])
        ones4 = P.sbuf("ones4", [128, 128], F32)
        P.op("dve", "memset", A(ones4.ap, 1.0), writes=[ones4])
        wlp = P.sbuf("wlp", [128, KC, NR], BF16)
        P.op("pool", "memset", A(wlp.ap, 0.0), writes=[wlp])
        rn = P.sbuf("rn", [128, TBK], F32)
        dg = P.sbuf("dg", [128, 12 * 4, 128], BF16)
        for i in range(12):
            for k in range(4):
                P.op("dve", "tensor_scalar", A(dg.ap[:, i * 4 + k, :], ident_f.ap, cw.ap[:, i, k:k + 1], None, ALU.mult),
                     reads=[ident_f, cw], writes=[dg])
        hb_big = P.sbuf("hb", [128, KC, TBK], BF16)
        hb = [T(hb_big.ap[:, k, :], f"hb{k}") for k in range(KC)]
        pjb = [P.sbuf(f"pjb{i}", [128, 4 + TBK], BF16) for i in range(12)]
        for i in range(12):
            P.op("pool", "memset", A(pjb[i].ap[:, 0:4], 0.0), writes=[pjb[i]])
        sz = [P.sbuf(f"sz{h}", [128, TBK], F32) for h in range(NH)]
        QT = [P.sbuf(f"QT{h}", [128, TBK], BF16) for h in range(NH)]
        KT = [P.sbuf(f"KT{h}", [128, TBK], BF16) for h in range(NH)]
        KTb = [P.sbuf(f"KTb{h}", [128, TBK], BF16) for h in range(NH)]
        QG = [P.sbuf(f"QG{h}", [128, TBK], BF16) for h in range(NH)]
        VT = [P.sbuf(f"VT{h}", [128, TBK], BF16) for h in range(NH)]
        gcr = [P.sbuf(f"gcr{h}", [128, TBK], F32) for h in range(NH)]
        egl = [P.sbuf(f"egl{h}", [128, NCK], F32) for h in range(NH)]
        scr = [P.sbuf(f"scr{i}", [128, TBK], F32) for i in range(3)]
        sqb = [P.sbuf(f"sqb{i}", [128, TBK], BF16) for i in range(2)]
        lg = P.sbuf("lg", [128, TBK], F32)
        gg = P.sbuf("gg", [128, TBK], F32)
        gc4 = P.sbuf("gc4", [128, TBK], F32)
        lt = scr
        rows = P.sbuf("rows", [128, TBK], F32)
        P.op("pool", "memset", A(rows.ap, 0.0), writes=[rows])
        cols = [P.sbuf(f"cols{c}", [128, 128], F32) for c in range(NCK)]
        ebg = [P.sbuf(f"ebg{c}", [128, 4], F32) for c in range(NCK)]
        S_f = [P.sbuf(f"Sf{h}", [128, 128], F32) for h in range(NH)]
        S_b = [P.sbuf(f"Sb{h}", [128, 128], BF16) for h in range(NH)]
        for h in range(NH):
            P.op("pool", "memset", A(S_f[h].ap, 0.0), writes=[S_f[h]])
            P.op("pool", "memset", A(S_b[h].ap, 0.0), writes=[S_b[h]])
        CW = int(os.environ.get("MK_CW", 2))
        NS = CW * NH
        def mk(name, dtype):
            return [P.sbuf(f"{name}{i}", [128, 128], dtype) for i in range(NS)]
        kbg, ktl, vb, dmt, dms, Bm, BmT, aT, Pm, uS, wT, vn, on = (
            mk("kbg", BF16), mk("ktl", BF16), mk("vb", BF16), mk("dmt", F32), mk("dms", F32),
            mk("Bm", BF16), mk("BmT", BF16), mk("aT", BF16), mk("Pm", BF16), mk("uS", F32), mk("wT", BF16),
            mk("vn", BF16), mk("on", BF16))
        Mk = [mk(f"Mk{k}_", BF16) for k in range(2)]
        MkT = [mk(f"MkT{k}_", BF16) for k in range(2)]
        ssum = [P.sbuf(f"ssum{i}", [128, 1], F32) for i in range(NS)]
        ost = [P.sbuf(f"ost{i}", [128, 128], BF16) for i in range(NS)]
        psA = bankF[0:2]
        psS = [V(t, t.ap[:, 0:128]) for t in bankF[2:6]] + [V(t, t.ap[:, 0:128]) for t in bankF[0:2]]
        psT = [V(t, t.ap[:, 0:128]) for t in bankB]
        pa = [0]
        pss = [0]
        pst = [0]

        def nextT():
            t = psT[pst[0] % len(psT)]
            pst[0] += 1
            return t

        def pt_b(t):
            return t.ap

        def nextA():
            t = psA[pa[0] % len(psA)]
            pa[0] += 1
            return t

        def nextS():
            t = psS[pss[0] % len(psS)]
            pss[0] += 1
            return t

        def mm(ps_view, lhsT, rhs, start=True, stop=True, reads=(), ps_tile=None):
            P.op("pe", "matmul", A(_ap(ps_view), _ap(lhsT), _ap(rhs), start=start, stop=stop),
                 reads=[lhsT, rhs] + list(reads), writes=[ps_tile if ps_tile is not None else ps_view])

        for tb in range(NTOK // TBK):
            t0 = tb * TBK
            if tb == 0:
                for k in range(KC):
                    P.dma("sp", hb[k], T(hT[k * 128:(k + 1) * 128, t0:t0 + TBK]))
            wl = load_w(w_gdn, 16 * 128, 8)
            P.op("dve", "tensor_copy", A(wlp.ap[:, :, 0:4], wl.ap[:, :, 4:8]), reads=[wl], writes=[wlp])
            P.op("dve", "tensor_copy", A(wlp.ap[:, :, 32:36], wl.ap[:, :, 0:4]), reads=[wl], writes=[wlp])
            P.op("dve", "tensor_copy", A(wlp.ap[:, :, 64:68], wl.ap[:, :, 4:8]), reads=[wl], writes=[wlp])
            for nb in range(NB):
                sl = slice(nb * 512, (nb + 1) * 512)
                psb = nextA()
                for k in range(KC):
                    P.op("pe", "matmul", A(psb.ap[0:NR, :], wlp.ap[:, k, :], hb[k].ap[:, nb * 512:(nb + 1) * 512], start=(k == 0), stop=(k == KC - 1)),
                         reads=[wlp, hb[k]], writes=[psb])
                P.op("act", "copy", A(lg.ap[0:NR, sl], psb.ap[0:NR, :]), reads=[psb], writes=[lg])
            R = slice(0, NR)
            P.op("dve", "tensor_scalar", A(lt[0].ap[R, :], lg.ap[R, :], hps.ap[R, 1:2], None, ALU.add), reads=[lg, hps], writes=[lt[0]])
            P.op("dve", "tensor_scalar", A(lt[1].ap[R, :], lt[0].ap[R, :], -1.0, None, ALU.mult), reads=[lt[0]], writes=[lt[1]])
            P.op("dve", "tensor_tensor", A(lt[1].ap[R, :], lt[1].ap[R, :], lt[0].ap[R, :], ALU.min), reads=[lt[0], lt[1]], writes=[lt[1]])
            P.op("act", "activation", A(lt[1].ap[R, :], lt[1].ap[R, :], AF.Exp), reads=[lt[1]], writes=[lt[1]])
            P.op("act", "activation", A(lt[1].ap[R, :], lt[1].ap[R, :], AF.Ln, bias=1.0), reads=[lt[1]], writes=[lt[1]])
            P.op("dve", "tensor_scalar", A(lt[2].ap[R, :], lt[0].ap[R, :], 0.0, None, ALU.max), reads=[lt[0]], writes=[lt[2]])
            P.op("dve", "tensor_tensor", A(lt[2].ap[R, :], lt[2].ap[R, :], lt[1].ap[R, :], ALU.add), reads=[lt[2], lt[1]], writes=[lt[2]])
            P.op("dve", "tensor_scalar", A(gg.ap[R, :], lt[2].ap[R, :], nA.ap[R, 0:1], None, ALU.mult), reads=[lt[2], nA], writes=[gg])
            for c in range(NCK):
                cs = slice(c * C, (c + 1) * C)
                P.op("dve", "tensor_tensor_scan", A(gc4.ap[R, cs], ones4.ap[R, :], gg.ap[R, cs], 0.0, ALU.mult, ALU.add), reads=[ones4, gg], writes=[gc4])
            P.op("dve", "tensor_copy", A(rows.ap[0:4, :], gc4.ap[0:4, :]), reads=[gc4], writes=[rows])
            P.op("act", "activation", A(rows.ap[32:36, :], lg.ap[32:36, :], AF.Sigmoid), reads=[lg], writes=[rows])
            for c in range(NCK):
                cs = slice(c * C, (c + 1) * C)
                P.op("act", "activation", A(rows.ap[64:68, cs], gc4.ap[64:68, cs], AF.Exp, bias=gc4.ap[64:68, c * C + C - 1:c * C + C], scale=-1.0),
                     reads=[gc4], writes=[rows])
            for c in range(NCK):
                cs = slice(c * C, (c + 1) * C)
                pt = nextS()
                P.op("pe", "transpose", A(pt.ap, rows.ap[:, cs], ident_f.ap), reads=[rows, ident_f], writes=[pt])
                P.op("act", "copy", A(cols[c].ap, pt.ap), reads=[pt], writes=[cols[c]])
                P.op("act", "activation", A(ebg[c].ap, cols[c].ap[:, 0:4], AF.Exp), reads=[cols[c]], writes=[ebg[c]])
                P.op("dve", "tensor_tensor", A(ebg[c].ap, ebg[c].ap, cols[c].ap[:, 32:36], ALU.mult), reads=[ebg[c], cols[c]], writes=[ebg[c]])
            if os.environ.get("MK_P2STOP") == "logits":
                o = T(oaT[0:128, t0:t0 + 128]); outs.append(o)
                P.op("dve", "tensor_copy", A(ost[0].ap, cols[0].ap), reads=[cols[0]], writes=[ost[0]])
                P.dma("sp", o, ost[0])
                continue
            def stageA(h):
                for ty in range(4):
                    wv = load_w(w_gdn, (h * 4 + ty) * 128)
                    for nb in range(NB):
                        ps = nextA()
                        for k in range(KC):
                            P.op("pe", "matmul", A(ps.ap, wv.ap[:, k, :], hb[k].ap[:, nb * 512:(nb + 1) * 512], start=(k == 0), stop=(k == KC - 1)),
                                 reads=[wv, hb[k]], writes=[ps])
                        sl = slice(nb * 512, (nb + 1) * 512)
                        if ty < 3:
                            pj = pjb[h * 3 + ty]
                            P.op("act", "copy", A(pj.ap[:, 4 + nb * 512:4 + (nb + 1) * 512], ps.ap), reads=[ps], writes=[pj])
                        else:
                            P.op("act", "activation", A(sz[h].ap[:, sl], ps.ap, AF.Silu), reads=[ps], writes=[sz[h]])
            def stageB(h):
                for nb in range(NB):
                    sl = slice(nb * 512, (nb + 1) * 512)
                    ps = nextA()
                    P.op("pe", "matmul", A(ps.ap, sel.ap[0:4, h, :], rows.ap[0:4, sl], start=True, stop=True), reads=[sel, rows], writes=[ps])
                    P.op("act", "copy", A(gcr[h].ap[:, sl], ps.ap), reads=[ps], writes=[gcr[h]])
                    ps2 = nextA()
                    P.op("pe", "matmul", A(ps2.ap, sel.ap[32:36, h, :], rows.ap[32:36, sl], start=True, stop=True), reads=[sel, rows], writes=[ps2])
                    P.op("act", "copy", A(scr[2].ap[:, sl], ps2.ap), reads=[ps2], writes=[scr[2]])
                for c in range(NCK):
                    P.op("act", "activation", A(egl[h].ap[:, c:c + 1], gcr[h].ap[:, c * C + C - 1:c * C + C], AF.Exp), reads=[gcr[h]], writes=[egl[h]])
                for ty in range(3):
                    pj = pjb[h * 3 + ty]
                    i = h * 3 + ty
                    for nb in range(NB):
                        ps = nextA()
                        for k in range(4):
                            P.op("pe", "matmul", A(ps.ap, dg.ap[:, i * 4 + k, :], pj.ap[:, 1 + k + nb * 512:1 + k + (nb + 1) * 512], start=(k == 0), stop=(k == 3)),
                                 reads=[dg, pj], writes=[ps])
                        sl = slice(nb * 512, (nb + 1) * 512)
                        if ty == 2:
                            P.op("act", "activation", A(VT[h].ap[:, sl], ps.ap, AF.Silu), reads=[ps], writes=[VT[h]])
                        else:
                            P.op("act", "activation", A(scr[ty].ap[:, sl], ps.ap, AF.Silu), reads=[ps], writes=[scr[ty]])
                    P.op("dve", "tensor_copy", A(pj.ap[:, 1:4], pj.ap[:, TBK + 1:TBK + 4]), reads=[pj], writes=[pj])
                for ty in range(2):
                    src = scr[ty]
                    for nb in range(NB):
                        sl = slice(nb * 512, (nb + 1) * 512)
                        sq = sqb[nb % 2]
                        P.op("act", "activation", A(sq.ap[:, 0:512], src.ap[:, sl], AF.Square), reads=[src], writes=[sq])
                        ps = nextA()
                        P.op("pe", "matmul", A(ps.ap, ones_b.ap, sq.ap[:, 0:512], start=True, stop=True), reads=[ones_b, sq], writes=[ps])
                        dstn = QT[h] if ty == 0 else KT[h]
                        P.op("dve", "tensor_scalar", A(rn.ap[:, sl], ps.ap, 1e-6, None, ALU.add), reads=[ps], writes=[rn])
                    recip_sqrt(rn.ap, rn)
                    scale = (HD ** -0.5) if ty == 0 else 1.0
                    P.op("dve", "scalar_tensor_tensor", A(dstn.ap, src.ap, scale, rn.ap, ALU.mult, ALU.mult), reads=[src, rn], writes=[dstn])
                P.op("dve", "tensor_tensor", A(KTb[h].ap, KT[h].ap, scr[2].ap, ALU.mult), reads=[KT[h], scr[2]], writes=[KTb[h]])
                P.op("act", "activation", A(rn.ap, gcr[h].ap, AF.Exp), reads=[gcr[h]], writes=[rn])
                P.op("dve", "tensor_tensor", A(QG[h].ap, QT[h].ap, rn.ap, ALU.mult), reads=[QT[h], rn], writes=[QG[h]])
            stageA(0)
            for h in range(NH):
                if h + 1 < NH:
                    stageA(h + 1)
                stageB(h)
            if tb + 1 < NTOK // TBK:
                for k in range(KC):
                    P.dma("sp", hb[k], T(hT[k * 128:(k + 1) * 128, t0 + TBK:t0 + 2 * TBK]))
            if os.environ.get("MK_P2STOP") == "proj":
                for h in range(NH):
                    o = T(oaT[h * 128:(h + 1) * 128, t0:t0 + TBK]); outs.append(o)
                    P.dma("sp", o, QG[h])
                continue
            for w0 in range(0, NCK, CW):
                probs = [(w0 + ci, h, ci * NH + h) for ci in range(CW) for h in range(NH)]

                def vw(t, c):
                    return V(t, t.ap[:, c * C:(c + 1) * C])
                for (c, h, s) in probs:
                    pt = nextT()
                    P.op("pe", "transpose", A(pt.ap, vw(KT[h], c).ap, ident_b.ap), reads=[KT[h], ident_b], writes=[pt])
                    P.op("act", "activation", A(kbg[s].ap, pt.ap, AF.Copy, scale=ebg[c].ap[:, h:h + 1]), reads=[pt, ebg[c]], writes=[kbg[s]])
                    P.op("dve", "tensor_scalar", A(ktl[s].ap, pt.ap, cols[c].ap[:, 64 + h:65 + h], None, ALU.mult), reads=[pt, cols[c]], writes=[ktl[s]])
                for (c, h, s) in probs:
                    pt2 = nextT()
                    P.op("pe", "transpose", A(pt2.ap, vw(VT[h], c).ap, ident_b.ap), reads=[VT[h], ident_b], writes=[pt2])
                    P.op("act", "activation", A(vb[s].ap, pt2.ap, AF.Copy, scale=cols[c].ap[:, 32 + h:33 + h]), reads=[pt2, cols[c]], writes=[vb[s]])
                for (c, h, s) in probs:
                    P.op("dve", "tensor_scalar", A(dmt[s].ap, gcr[h].ap[:, c * C:(c + 1) * C], cols[c].ap[:, h:h + 1], 0.0, ALU.subtract, ALU.min), reads=[gcr[h], cols[c]], writes=[dmt[s]])
                for (c, h, s) in probs:
                    P.op("act", "activation", A(dmt[s].ap, dmt[s].ap, AF.Exp), reads=[dmt[s]], writes=[dmt[s]])
                for (c, h, s) in probs:
                    P.op("dve", "tensor_tensor", A(dms[s].ap, dmt[s].ap, masks.ap[:, 0, :], ALU.mult), reads=[dmt[s], masks], writes=[dms[s]])
                    P.op("dve", "tensor_tensor", A(dmt[s].ap, dmt[s].ap, masks.ap[:, 1, :], ALU.mult), reads=[dmt[s], masks], writes=[dmt[s]])
                for (c, h, s) in probs:
                    p1 = nextS()
                    mm(p1.ap, vw(KT[h], c), vw(KTb[h], c), ps_tile=p1)
                    P.op("dve", "tensor_tensor", A(Bm[s].ap, p1.ap, dms[s].ap, ALU.mult), reads=[p1, dms[s]], writes=[Bm[s]])
                for (c, h, s) in probs:
                    p2 = nextS()
                    mm(p2.ap, vw(KT[h], c), vw(QT[h], c), ps_tile=p2)
                    P.op("dve", "tensor_tensor", A(aT[s].ap, p2.ap, dmt[s].ap, ALU.mult), reads=[p2, dmt[s]], writes=[aT[s]])
                for (c, h, s) in probs:
                    p3 = nextT()
                    P.op("pe", "transpose", A(p3.ap, Bm[s].ap, ident_b.ap), reads=[Bm[s], ident_b], writes=[p3])
                    P.op("act", "copy", A(BmT[s].ap, p3.ap), reads=[p3], writes=[BmT[s]])
                    P.op("dve", "tensor_tensor", A(Pm[s].ap, Bm[s].ap, ident_f.ap, ALU.add), reads=[Bm[s], ident_f], writes=[Pm[s]])
                cur = {s: (Bm[s], BmT[s]) for (_, _, s) in probs}
                for it in range(6):
                    for (c, h, s) in probs:
                        M, MT = cur[s]
                        pq = nextS()
                        mm(pq.ap, M, MT, ps_tile=pq)
                        P.op("act", "copy", A(MkT[it % 2][s].ap, pq.ap), reads=[pq], writes=[MkT[it % 2][s]])
                    if it < 5:
                        for (c, h, s) in probs:
                            M, MT = cur[s]
                            pr = nextS()
                            mm(pr.ap, MT, M, ps_tile=pr)
                            P.op("dve", "tensor_copy", A(Mk[it % 2][s].ap, pr.ap), reads=[pr], writes=[Mk[it % 2][s]])
                    for (c, h, s) in probs:
                        pp = nextS()
                        mm(pp.ap, ident_b, Pm[s], start=True, stop=False, ps_tile=pp)
                        mm(pp.ap, MkT[it % 2][s], Pm[s], start=False, stop=True, ps_tile=pp)
                        P.op("dve" if s % 2 == 0 else "act", "tensor_copy" if s % 2 == 0 else "copy", A(Pm[s].ap, pp.ap), reads=[pp], writes=[Pm[s]])
                    for (c, h, s) in probs:
                        cur[s] = (Mk[it % 2][s], MkT[it % 2][s])
                for (c, h, s) in probs:
                    pu = nextS()
                    mm(pu.ap, Pm[s], vb[s], ps_tile=pu)
                    P.op("act", "copy", A(uS[s].ap, pu.ap), reads=[pu], writes=[uS[s]])
                for (c, h, s) in probs:
                    pw = nextS()
                    mm(pw.ap, kbg[s], Pm[s], ps_tile=pw)
                    P.op("dve", "tensor_copy", A(wT[s].ap, pw.ap), reads=[pw], writes=[wT[s]])
                for ci in range(CW):
                    pl = [(c, h, s) for (c, h, s) in probs if c == w0 + ci]
                    for (c, h, s) in pl:
                        pv = nextS()
                        mm(pv.ap, wT[s], S_b[h], ps_tile=pv)
                        P.op("dve", "tensor_tensor", A(vn[s].ap, uS[s].ap, pv.ap, ALU.subtract), reads=[uS[s], pv], writes=[vn[s]])
                    for (c, h, s) in pl:
                        po = nextS()
                        mm(po.ap, vw(QG[h], c), S_b[h], start=True, stop=False, ps_tile=po)
                        mm(po.ap, aT[s], vn[s], start=False, stop=True, ps_tile=po)
                        P.op("act", "copy", A(uS[s].ap, po.ap), reads=[po], writes=[uS[s]])
                    for (c, h, s) in pl:
                        pS = nextS()
                        mm(pS.ap, ktl[s], vn[s], ps_tile=pS)
                        P.op("dve", "scalar_tensor_tensor", A(S_f[h].ap, S_f[h].ap, egl[h].ap[:, c:c + 1], pS.ap, ALU.mult, ALU.add),
                             reads=[S_f[h], egl[h], pS], writes=[S_f[h]])
                        P.op("act", "copy", A(S_b[h].ap, S_f[h].ap), reads=[S_f[h]], writes=[S_b[h]])
                for (c, h, s) in probs:
                    P.op("act", "activation", A(dms[s].ap, uS[s].ap, AF.Square, accum_out=ssum[s].ap), reads=[uS[s]], writes=[dms[s], ssum[s]])
                for (c, h, s) in probs:
                    P.op("dve", "tensor_scalar", A(ssum[s].ap, ssum[s].ap, 1.0 / HD, RMS_EPS, ALU.mult, ALU.add), reads=[ssum[s]], writes=[ssum[s]])
                for (c, h, s) in probs:
                    P.op("act", "activation", A(ssum[s].ap, ssum[s].ap, AF.Sqrt), reads=[ssum[s]], writes=[ssum[s]])
                for (c, h, s) in probs:
                    P.op("dve", "reciprocal", A(ssum[s].ap, ssum[s].ap), reads=[ssum[s]], writes=[ssum[s]])
                for (c, h, s) in probs:
                    P.op("act", "activation", A(on[s].ap, uS[s].ap, AF.Copy, scale=ssum[s].ap[:, 0:1]), reads=[uS[s], ssum[s]], writes=[on[s]])
                for (c, h, s) in probs:
                    pz = nextT()
                    P.op("pe", "transpose", A(pz.ap, on[s].ap, ident_b.ap), reads=[on[s], ident_b], writes=[pz])
                    P.op("dve", "scalar_tensor_tensor", A(ost[s].ap, pz.ap, gn.ap[:, 0:1], sz[h].ap[:, c * C:(c + 1) * C], ALU.mult, ALU.mult),
                         reads=[pz, gn, sz[h]], writes=[ost[s]])
                    o = T(oaT[h * HD:(h + 1) * HD, t0 + c * C:t0 + (c + 1) * C])
                    outs.append(o)
                    P.dma("sp", o, ost[s])
    P.wait_tile("sp", outs)
    P.emit()
    P.close()
    return P


import ml_dtypes
from concourse.bass_utils import run_bass_kernel_spmd

_BF = ml_dtypes.bfloat16
_PROGS = {}
NTC_CORE = 1024
SEQ = 4096
NCORES = 8


def _prog(name, builder):
    if name not in _PROGS:
        nc = bass.Bass("TRN2", target_bir_lowering=False)
        builder(nc)
        _PROGS[name] = nc
    return _PROGS[name]


def _nl(vs):
    return np.ascontiguousarray(np.stack([np.asarray(v, np.float32).reshape(KC, 128).T for v in vs], axis=1))


def _consts():
    ident = np.eye(128, dtype=np.float32)
    jj, ii = np.meshgrid(np.arange(128), np.arange(128), indexing="ij")
    masks = np.ascontiguousarray(np.stack([-(ii > jj).astype(np.float32), (ii >= jj).astype(np.float32)], axis=1))
    sel = np.zeros((128, 4, 128), np.float32)
    for hh in range(4):
        sel[hh, hh, :] = 1.0
        sel[32 + hh, hh, :] = 1.0
    return ident, masks, sel


def _p2_inputs(l, b, r, hT_full, hT_own, inp, consts):
    w_in = inp["w_in"][l]
    cols = []
    for hl in range(4):
        head = 4 * r + hl
        for ty in range(3):
            cols.append(np.arange(ty * 2048 + head * 128, ty * 2048 + (head + 1) * 128))
        cols.append(np.arange(6144 + head * 128, 6144 + (head + 1) * 128))
    cols.append(np.arange(8192 + 4 * r, 8192 + 4 * r + 4))
    cols.append(np.arange(8208 + 4 * r, 8208 + 4 * r + 4))
    w_gdn = np.ascontiguousarray(w_in[:, np.concatenate(cols)])
    gcw = inp["gdn_conv_w"][l]
    convw = np.zeros((128, 12, 4), np.float32)
    for hl in range(4):
        head = 4 * r + hl
        for ty in range(3):
            convw[:, hl * 3 + ty, :] = gcw[:, ty * 2048 + head * 128: ty * 2048 + (head + 1) * 128].T
    hp = np.zeros((128, 2), np.float32)
    for base in (0, 64):
        hp[base:base + 4, 0] = inp["gdn_a_log"][l][4 * r:4 * r + 4]
        hp[base:base + 4, 1] = inp["gdn_dt_bias"][l][4 * r:4 * r + 4]
    cvp = np.ascontiguousarray(np.concatenate([
        inp["cm_pw_b"][l].reshape(32, 128).T, inp["cm_dw_b"][l].reshape(16, 128).T,
        inp["cm_ln_w"][l].reshape(16, 128).T, inp["cm_ln_b"][l].reshape(16, 128).T], axis=1).astype(np.float32))
    dww = np.ascontiguousarray(inp["cm_dw_w"][l].reshape(31, 16, 128).transpose(2, 1, 0))
    if r == 0:
        halo = np.zeros((D, 32), _BF)
    else:
        halo = hT_full[:, NTC_CORE * r - 32:NTC_CORE * r]
    hT_cv = np.ascontiguousarray(np.concatenate([halo, hT_own], axis=1))
    ident, masks, sel = consts
    return {"hT": hT_full, "hT_cv": hT_cv, "w_gdn": w_gdn, "convw": convw, "hp": hp,
            "gnw": np.ascontiguousarray(inp["gdn_norm_w"][l].reshape(128, 1).astype(np.float32)),
            "cvp": cvp, "dww": dww, "halo_mask": np.full((128, 1), 0.0 if r == 0 else 1.0, np.float32),
            "ident_f": ident, "masks": masks, "sel": sel}


def kernel(**inputs):
    inp = {k: np.asarray(v) for k, v in inputs.items()}
    x = inp["x"].astype(np.float32, copy=False)
    cores = [(b, r) for b in range(2) for r in range(4)]
    ids = list(range(NCORES))
    consts = _consts()
    xT = [np.ascontiguousarray(x[b, r * NTC_CORE:(r + 1) * NTC_CORE, :].T) for (b, r) in cores]
    nc1 = _prog("p1", lambda nc: build_p1(nc, NTC_CORE))
    nc2 = _prog("p2", lambda nc: build_p2(nc, SEQ, NCV=NTC_CORE))
    nc3 = _prog("p3", lambda nc: build_p3(nc, NTC_CORE))
    n0 = _nl([inp["pre_mix_norm"][0]])
    res = run_bass_kernel_spmd(nc1, [{"xT": xT[c], "norms": n0} for c in ids], core_ids=ids)
    hT = [res.results[c]["h_out"] for c in ids]
    depth = inp["w_in"].shape[0]
    for l in range(depth):
        hT_full = [np.ascontiguousarray(np.concatenate([hT[b * 4 + r] for r in range(4)], axis=1)) for b in range(2)]
        w_cv = np.ascontiguousarray(inp["w_in"][l][:, 8224:])
        maps = []
        for c, (b, r) in enumerate(cores):
            m = _p2_inputs(l, b, r, hT_full[b], hT[c], inp, consts)
            m["w_cv"] = w_cv
            maps.append(m)
        res = run_bass_kernel_spmd(nc2, maps, core_ids=ids)
        oaT = [res.results[c]["oaT"] for c in ids]
        cT = [res.results[c]["cT"] for c in ids]
        del maps
        nxt = inp["pre_mix_norm"][l + 1] if l + 1 < depth else np.ones(D, np.float32)
        nrm = _nl([inp["post_mix_norm"][l], inp["pre_ffn_norm"][l], inp["post_ffn_norm"][l], nxt])
        maps = []
        for c, (b, r) in enumerate(cores):
            mixT = np.ascontiguousarray(np.concatenate(
                [oaT[b * 4 + rr][:, r * NTC_CORE:(r + 1) * NTC_CORE] for rr in range(4)] + [cT[c]], axis=0))
            maps.append({"mixT": mixT, "xT": xT[c], "w_out": inp["w_out"][l], "w_gate": inp["w_gate"][l],
                         "w_up": inp["w_up"][l], "w_down": inp["w_down"][l], "norms": nrm})
        res = run_bass_kernel_spmd(nc3, maps, core_ids=ids)
        del maps
        xT = [res.results[c]["x_out"] for c in ids]
        hT = [res.results[c]["h_out"] for c in ids]
    out = np.empty_like(x)
    for c, (b, r) in enumerate(cores):
        out[b, r * NTC_CORE:(r + 1) * NTC_CORE, :] = xT[c].T
    return out
```

```python
import os
import numpy as np
import concourse.bass as bass
import concourse.mybir as mybir

F32 = mybir.dt.float32
BF16 = mybir.dt.bfloat16
AF = mybir.ActivationFunctionType
ALU = mybir.AluOpType
AX = mybir.AxisListType

ENGS = ("pe", "act", "dve", "pool", "sp")
SEM_EPOCH = 1 << 30


class A:
    def __init__(self, *a, **k):
        self.a = a
        self.k = k


class T:
    __slots__ = ("ap", "name", "w", "wh", "rh", "excl")

    def __init__(self, ap, name="", excl=False):
        self.ap = ap
        self.name = name
        self.excl = excl
        self.w = None
        self.wh = {}
        self.rh = {}

    def __getitem__(self, idx):
        return V(self, self.ap[idx])


class V:
    __slots__ = ("t", "ap")

    def __init__(self, t, ap):
        self.t = t
        self.ap = ap

    def __getitem__(self, idx):
        return V(self.t, self.ap[idx])


def _tile(x):
    return x.t if isinstance(x, V) else x


def _ap(x):
    return x.ap if isinstance(x, (V,)) else (x.ap if isinstance(x, T) else x)


class Prog:
    def __init__(self, nc, same_engine_sync=True, n_dma_sems=24):
        self.nc = nc
        self.es = nc
        self.lists = {e: [] for e in ENGS}
        self.cnt = {e: 0 for e in ENGS}
        self.epoch = {e: 0 for e in ENGS}
        self.sems = {e: [] for e in ENGS}
        self.waited = {e: {} for e in ENGS}
        self.same = same_engine_sync
        self.ctx = []
        self._sem_ctx = []
        self.dma_sems = {}
        self.dma_rr = {}
        self.n_dma_sems = n_dma_sems
        self.n_inst = 0

    def _enter(self, cm):
        v = cm.__enter__()
        self.ctx.append(cm)
        return v

    def close(self):
        for cm in reversed(self.ctx):
            cm.__exit__(None, None, None)
        self.ctx = []
        for cm in reversed(self._sem_ctx):
            cm.__exit__(None, None, None)
        self._sem_ctx = []

    def mark(self):
        return len(self.ctx)

    def release(self, mark):
        while len(self.ctx) > mark:
            self.ctx.pop().__exit__(None, None, None)

    def barrier(self):
        for e in ENGS:
            waits = {}
            for f in ENGS:
                if f != e and self.cnt[f] > 0:
                    self._need(e, ("eng", f, self.epoch[f], self.cnt[f]), waits)
            for q, pool in self.dma_sems.items():
                for slot in pool:
                    if slot[1] > 0:
                        self._need(e, ("dma", (q, slot[2]), slot[0], slot[1]), waits)
            self._emit_waits(e, waits)

    def sem(self, name):
        cm = self.nc.semaphore(name)
        v = cm.__enter__()
        self._sem_ctx.append(cm)
        return v

    def sbuf(self, name, shape, dtype):
        t = self._enter(self.nc.sbuf_tensor("s_" + name, list(shape), dtype))
        return T(t[:], name)

    def psum(self, name, shape, dtype):
        t = self._enter(self.nc.psum_tensor("p_" + name, list(shape), dtype))
        return T(t[:], name, excl=True)

    def dram(self, ap, name=""):
        return T(ap, name)

    def _eng_sem(self, e):
        ep = self.epoch[e]
        while len(self.sems[e]) <= ep:
            self.sems[e].append(self.sem(f"p_{e}_{len(self.sems[e])}"))
        return self.sems[e][ep]

    def _need(self, e, tok, waits):
        if tok is None:
            return
        if tok[0] == "eng":
            _, pe_, ep, c = tok
            if pe_ == e and (not self.same or e == "pe"):
                return
            key = ("eng", pe_, ep)
            sem = self.sems[pe_][ep]
        else:
            _, sid, sem, c = tok
            key = ("dma", sid)
        if self.waited[e].get(key, 0) >= c:
            return
        prev = waits.get(key)
        if prev is None or prev[1] < c:
            waits[key] = (sem, c)

    def _collect(self, e, reads, writes):
        waits = {}
        for r in reads:
            t = _tile(r)
            for tok in t.wh.values():
                self._need(e, tok, waits)
            if t.excl:
                for tok in t.rh.values():
                    self._need(e, tok, waits)
        for w in writes:
            t = _tile(w)
            for tok in t.wh.values():
                self._need(e, tok, waits)
            for tok in t.rh.values():
                self._need(e, tok, waits)
        return waits

    def _emit_waits(self, e, waits):
        for key, (sem, c) in waits.items():
            self.waited[e][key] = c
            self.lists[e].append(("wait", sem, c))

    @staticmethod
    def _key(tok):
        return ("eng", tok[1], tok[2]) if tok[0] == "eng" else ("dma", tok[1])

    def _mark(self, tok, reads, writes):
        k = self._key(tok)
        for r in reads:
            _tile(r).rh[k] = tok
        for w in writes:
            t = _tile(w)
            t.w = tok
            t.wh[k] = tok

    def op(self, e, meth, args=(), reads=(), writes=(), **kw):
        if callable(meth):
            fn = meth
        else:
            fn = (lambda eng, meth=meth, a=args.a, k=args.k: getattr(eng, meth)(*a, **k))
        waits = self._collect(e, reads, writes)
        self._emit_waits(e, waits)
        if self.cnt[e] >= SEM_EPOCH:
            self.epoch[e] += 1
            self.cnt[e] = 0
        sem = self._eng_sem(e)
        self.cnt[e] += 1
        tok = ("eng", e, self.epoch[e], self.cnt[e])
        self.lists[e].append(("inst", fn, sem, 1))
        self._mark(tok, reads, writes)
        self.n_inst += 1
        return tok

    def dma(self, q, out, in_, **kw):
        waits = self._collect(q, [in_], [out])
        pool = self.dma_sems.setdefault(q, [])
        if len(pool) < self.n_dma_sems:
            pool.append([self.sem(f"d_{q}_{len(pool)}"), 0, len(pool)])
            slot = pool[-1]
        else:
            i = self.dma_rr.get(q, 0)
            self.dma_rr[q] = (i + 1) % len(pool)
            slot = pool[i]
            if slot[1] > 0:
                self._need(q, ("dma", (q, slot[2]), slot[0], slot[1]), waits)
        self._emit_waits(q, waits)
        slot[1] += 16
        tok = ("dma", (q, slot[2]), slot[0], slot[1])
        o_ap, i_ap = _ap(out), _ap(in_)
        self.lists[q].append(("inst", lambda eng: eng.dma_start(out=o_ap, in_=i_ap, **kw), slot[0], 16))
        self._mark(tok, [in_], [out])
        return tok

    def dma_like(self, q, fn, reads, writes):
        waits = self._collect(q, reads, writes)
        pool = self.dma_sems.setdefault(q, [])
        if len(pool) < self.n_dma_sems:
            pool.append([self.sem(f"d_{q}_{len(pool)}"), 0, len(pool)])
            slot = pool[-1]
        else:
            i = self.dma_rr.get(q, 0)
            self.dma_rr[q] = (i + 1) % len(pool)
            slot = pool[i]
            if slot[1] > 0:
                self._need(q, ("dma", (q, slot[2]), slot[0], slot[1]), waits)
        self._emit_waits(q, waits)
        slot[1] += 16
        tok = ("dma", (q, slot[2]), slot[0], slot[1])
        self.lists[q].append(("inst", fn, slot[0], 16))
        self._mark(tok, reads, writes)
        return tok

    def wait_tile(self, e, tiles):
        waits = {}
        for t in tiles:
            self._need(e, _tile(t).w, waits)
        self._emit_waits(e, waits)

    def emit(self):
        nc = self.nc
        lists = self.lists

        def run(eng, items):
            for it in items:
                if it[0] == "wait":
                    eng.wait_ge(it[1], it[2])
                else:
                    ins = it[1](eng)
                    ins.then_inc(it[2], it[3])

        with nc.Block() as block:
            @block.tensor
            def _(eng):
                run(eng, lists["pe"])

            @block.scalar
            def _(eng):
                run(eng, lists["act"])

            @block.vector
            def _(eng):
                run(eng, lists["dve"])

            @block.gpsimd
            def _(eng):
                run(eng, lists["pool"])

            @block.sync
            def _(eng):
                run(eng, lists["sp"])


D = int(os.environ.get("MK_D", 4096))
DFF = int(os.environ.get("MK_DFF", 11008))
KC = D // 128
FC = DFF // 128
HALF = FC // 2
EPS = 1e-6


class Ctx:
    pass


def setup_common(P, TB):
    c = Ctx()
    c.P = P
    c.TB = TB
    c.ones = P.sbuf("ones", [128, 128], BF16)
    P.op("dve", "memset", A(c.ones.ap, 1.0), writes=[c.ones])
    c.NW = 4
    c.wbig = P.sbuf("wbuf", [128, c.NW, max(HALF, KC) * 128], BF16)
    c.wb = [T(c.wbig.ap[:, i, :], f"wb{i}") for i in range(c.NW)]
    c.wi = 0
    c.ps = [P.psum(f"ps{i}", [128, 512], F32) for i in range(7)]
    c.pi = 0
    c.ss = P.psum("ss", [128, 512], F32)
    c.sq = [P.sbuf(f"sq{i}", [128, 512], BF16) for i in range(2)]
    c.sqi = 0
    c.rstd = P.sbuf("rstd", [128, 512], F32)
    return c


def load_w(c, W, k0, nkc, m0, mw=128):
    P = c.P
    buf = c.wb[c.wi % c.NW]
    c.wi += 1
    dst = V(buf, buf.ap[:, 0:nkc * mw].rearrange("p (k m) -> p k m", m=mw))
    if getattr(c, "tiled", False):
        blk = (m0 // 128) if k0 == 0 else (KC + m0 // 128)
        P.dma("pool", V(buf, buf.ap[:, 0:nkc * mw]), T(W[blk]))
        return dst
    src = T(W[k0 * 128:(k0 + nkc) * 128, m0:m0 + mw].rearrange("(k p) m -> p k m", p=128))
    P.dma("pool", dst, src)
    return dst


def next_ps(c):
    t = c.ps[c.pi % len(c.ps)]
    c.pi += 1
    return t


def mm_acc(c, ps, wv, acts, n, first=True, last=True):
    P = c.P
    nk = len(acts)
    for k in range(nk):
        a = acts[k]
        P.op("pe", "matmul", A(ps.ap[:, 0:n], wv.ap[:, k, :], _ap(a), start=(first and k == 0), stop=(last and k == nk - 1)),
             reads=[wv, a], writes=[ps])


def sumsq_step(c, src, n, first, last):
    P = c.P
    sq = c.sq[c.sqi % 2]
    c.sqi += 1
    P.op("act", "activation", A(sq.ap[:, 0:n], _ap(src), AF.Square), reads=[src], writes=[sq])
    P.op("pe", "matmul", A(c.ss.ap[:, 0:n], c.ones.ap, sq.ap[:, 0:n], start=first, stop=last), reads=[c.ones, sq], writes=[c.ss])


def finish_rstd(c, n, dim):
    P = c.P
    P.op("dve", "tensor_scalar", A(c.rstd.ap[:, 0:n], c.ss.ap[:, 0:n], 1.0 / dim, EPS, ALU.mult, ALU.add), reads=[c.ss], writes=[c.rstd])
    P.op("act", "activation", A(c.rstd.ap[:, 0:n], c.rstd.ap[:, 0:n], AF.Sqrt), reads=[c.rstd], writes=[c.rstd])
    P.op("dve", "reciprocal", A(c.rstd.ap[:, 0:n], c.rstd.ap[:, 0:n]), reads=[c.rstd], writes=[c.rstd])


def build_p3(nc, NTOK, TB=512, want_h=True):
    P = Prog(nc)
    dt = nc.dram_tensor
    mixT = dt("mixT", [D, NTOK], BF16, kind="ExternalInput").ap()
    xT = dt("xT", [D, NTOK], F32, kind="ExternalInput").ap()
    TILED = os.environ.get("MK_WTILED", "0") == "1"
    if TILED:
        w_out = dt("w_out", [KC, 128, KC * 128], BF16, kind="ExternalInput").ap()
        w_gate = dt("w_gate", [FC, 128, KC * 128], BF16, kind="ExternalInput").ap()
        w_up = dt("w_up", [FC, 128, KC * 128], BF16, kind="ExternalInput").ap()
        w_down = dt("w_down", [2 * KC, 128, HALF * 128], BF16, kind="ExternalInput").ap()
    else:
        w_out = dt("w_out", [D, D], F32, kind="ExternalInput").ap()
        w_gate = dt("w_gate", [D, DFF], F32, kind="ExternalInput").ap()
        w_up = dt("w_up", [D, DFF], F32, kind="ExternalInput").ap()
        w_down = dt("w_down", [DFF, D], F32, kind="ExternalInput").ap()
    c_tiled = TILED
    norms = dt("norms", [128, 4, KC], F32, kind="ExternalInput").ap()
    x_out = dt("x_out", [D, NTOK], F32, kind="ExternalOutput").ap()
    h_out = dt("h_out", [D, NTOK], BF16, kind="ExternalOutput").ap()
    x1d = dt("x1_scratch", [D, NTOK], F32, kind="ExternalOutput").ap()

    c = setup_common(P, TB)
    c.tiled = c_tiled
    nrm = P.sbuf("nrm", [128, 4, KC], F32)
    P.dma("sp", nrm, T(norms))
    mh_big = P.sbuf("mh", [128, KC, TB], BF16)
    mh = [T(mh_big.ap[:, k, :], f"mh{k}") for k in range(KC)]
    y_big = P.sbuf("y", [128, KC, TB], F32)
    y = [T(y_big.ap[:, k, :], f"y{k}") for k in range(KC)]
    act_big = P.sbuf("act", [128, HALF, TB], BF16)
    act = [T(act_big.ap[:, k, :], f"act{k}") for k in range(HALF)]
    stg = [P.sbuf(f"stg{i}", [128, TB], F32) for i in range(2)]
    sg = [P.sbuf(f"sg{i}", [128, TB], F32) for i in range(2)]
    hstg = [P.sbuf(f"hstg{i}", [128, TB], BF16) for i in range(2)]
    x1T = T(x1d)
    outs = []

    def feat(ap, m, t0):
        return ap[m * 128:(m + 1) * 128, t0:t0 + TB]

    for tb in range(NTOK // TB):
        t0 = tb * TB
        for k in range(KC):
            P.dma("sp", mh[k], T(feat(mixT, k, t0)))
        for m in range(KC):
            wv = load_w(c, w_out, 0, KC, m * 128)
            ps = next_ps(c)
            mm_acc(c, ps, wv, mh, TB)
            P.op("act", "copy", A(y[m].ap, ps.ap), reads=[ps], writes=[y[m]])
            sumsq_step(c, ps, TB, m == 0, m == KC - 1)
        if os.environ.get("MK_STOP") == "b":
            for m in range(KC):
                o = T(feat(x_out, m, t0)); outs.append(o)
                P.dma("sp", o, y[m])
            continue
        finish_rstd(c, TB, D)
        if os.environ.get("MK_STOP") == "c":
            for m in range(KC):
                o = T(feat(x_out, m, t0)); outs.append(o)
                P.dma("sp", o, y[m])
            o = T(x_out[0:128, 0:TB]); outs.append(o)
            P.dma("sp", o, c.rstd)
            continue
        P.dma("sp", stg[0], T(feat(xT, 0, t0)))
        for m in range(KC):
            st = stg[m % 2]
            if m + 1 < KC:
                P.dma("sp", stg[(m + 1) % 2], T(feat(xT, m + 1, t0)))
            P.op("dve", "scalar_tensor_tensor", A(y[m].ap, y[m].ap, nrm.ap[:, 0, m:m + 1], c.rstd.ap, ALU.mult, ALU.mult),
                 reads=[y[m], nrm, c.rstd], writes=[y[m]])
            P.op("dve", "tensor_tensor", A(y[m].ap, y[m].ap, st.ap, ALU.add), reads=[y[m], st], writes=[y[m]])
            P.dma("sp", V(x1T, feat(x1d, m, t0)), y[m])
            sumsq_step(c, y[m], TB, m == 0, m == KC - 1)
        finish_rstd(c, TB, D)
        for m in range(KC):
            P.op("dve", "scalar_tensor_tensor", A(mh[m].ap, y[m].ap, nrm.ap[:, 1, m:m + 1], c.rstd.ap, ALU.mult, ALU.mult),
                 reads=[y[m], nrm, c.rstd], writes=[mh[m]])
        for half in range(2):
            for j in range(HALF):
                col = (half * HALF + j) * 128
                wg = load_w(c, w_gate, 0, KC, col)
                psg = next_ps(c)
                mm_acc(c, psg, wg, mh, TB)
                wu = load_w(c, w_up, 0, KC, col)
                psu = next_ps(c)
                mm_acc(c, psu, wu, mh, TB)
                s = sg[j % 2]
                P.op("act", "activation", A(s.ap, psg.ap, AF.Silu), reads=[psg], writes=[s])
                P.op("dve", "tensor_tensor", A(act[j].ap, s.ap, psu.ap, ALU.mult), reads=[s, psu], writes=[act[j]])
            for m in range(KC):
                wv = load_w(c, w_down, half * HALF, HALF, m * 128)
                ps = next_ps(c)
                mm_acc(c, ps, wv, act, TB)
                if half == 0:
                    P.op("act", "copy", A(y[m].ap, ps.ap), reads=[ps], writes=[y[m]])
                else:
                    P.op("dve", "tensor_tensor", A(y[m].ap, y[m].ap, ps.ap, ALU.add), reads=[ps, y[m]], writes=[y[m]])
                    sumsq_step(c, y[m], TB, m == 0, m == KC - 1)
        finish_rstd(c, TB, D)
        P.dma("sp", stg[0], V(x1T, feat(x1d, 0, t0)))
        for m in range(KC):
            st = stg[m % 2]
            if m + 1 < KC:
                P.dma("sp", stg[(m + 1) % 2], V(x1T, feat(x1d, m + 1, t0)))
            P.op("dve", "scalar_tensor_tensor", A(y[m].ap, y[m].ap, nrm.ap[:, 2, m:m + 1], c.rstd.ap, ALU.mult, ALU.mult),
                 reads=[y[m], nrm, c.rstd], writes=[y[m]])
            P.op("dve", "tensor_tensor", A(y[m].ap, y[m].ap, st.ap, ALU.add), reads=[y[m], st], writes=[y[m]])
            o = T(feat(x_out, m, t0))
            outs.append(o)
            P.dma("sp", o, y[m])
            if want_h:
                sumsq_step(c, y[m], TB, m == 0, m == KC - 1)
        if want_h:
            finish_rstd(c, TB, D)
            for m in range(KC):
                hs = hstg[m % 2]
                P.op("dve", "scalar_tensor_tensor", A(hs.ap, y[m].ap, nrm.ap[:, 3, m:m + 1], c.rstd.ap, ALU.mult, ALU.mult),
                     reads=[y[m], nrm, c.rstd], writes=[hs])
                o = T(feat(h_out, m, t0))
                outs.append(o)
                P.dma("sp", o, hs)
    P.wait_tile("sp", outs)
    P.emit()
    P.close()
    return P


def build_p1(nc, NTOK, TB=512):
    P = Prog(nc)
    dt = nc.dram_tensor
    xT = dt("xT", [D, NTOK], F32, kind="ExternalInput").ap()
    norms = dt("norms", [128, 1, KC], F32, kind="ExternalInput").ap()
    h_out = dt("h_out", [D, NTOK], BF16, kind="ExternalOutput").ap()
    c = Ctx()
    c.P = P
    c.ones = P.sbuf("ones", [128, 128], BF16)
    P.op("dve", "memset", A(c.ones.ap, 1.0), writes=[c.ones])
    c.ss = P.psum("ss", [128, 512], F32)
    c.sq = [P.sbuf(f"sq{i}", [128, 512], BF16) for i in range(2)]
    c.sqi = 0
    c.rstd = P.sbuf("rstd", [128, 512], F32)
    nrm = P.sbuf("nrm", [128, 1, KC], F32)
    P.dma("sp", nrm, T(norms))
    y_big = P.sbuf("y", [128, KC, TB], F32)
    y = [T(y_big.ap[:, k, :], f"y{k}") for k in range(KC)]
    hstg = [P.sbuf(f"hstg{i}", [128, TB], BF16) for i in range(2)]
    outs = []
    for tb in range(NTOK // TB):
        t0 = tb * TB
        for m in range(KC):
            P.dma("sp", y[m], T(xT[m * 128:(m + 1) * 128, t0:t0 + TB]))
            sumsq_step(c, y[m], TB, m == 0, m == KC - 1)
        finish_rstd(c, TB, D)
        for m in range(KC):
            hs = hstg[m % 2]
            P.op("dve", "scalar_tensor_tensor", A(hs.ap, y[m].ap, nrm.ap[:, 0, m:m + 1], c.rstd.ap, ALU.mult, ALU.mult),
                 reads=[y[m], nrm, c.rstd], writes=[hs])
            o = T(h_out[m * 128:(m + 1) * 128, t0:t0 + TB])
            outs.append(o)
            P.dma("sp", o, hs)
    P.wait_tile("sp", outs)
    P.emit()
    P.close()
    return P


D = int(os.environ.get("MK_D", 4096))
KC = D // 128
HD = 128
C = 128
NH = 4
NCH = 16
DWK = 31
RMS_EPS = 1e-6
LN_EPS = 1e-5


def build_p2(nc, NTOK, TBK=512, NCV=1024, do_gdn=True, do_cv=True):
    P = Prog(nc)
    dt = nc.dram_tensor
    hT = dt("hT", [D, NTOK], BF16, kind="ExternalInput").ap()
    hT_cv = dt("hT_cv", [D, 32 + NCV], BF16, kind="ExternalInput").ap()
    w_gdn = dt("w_gdn", [D, 16 * 128 + 8], F32, kind="ExternalInput").ap()
    w_cv = dt("w_cv", [D, 4096], F32, kind="ExternalInput").ap()
    convw = dt("convw", [128, 12, 4], F32, kind="ExternalInput").ap()
    hp = dt("hp", [128, 2], F32, kind="ExternalInput").ap()
    gnw = dt("gnw", [128, 1], F32, kind="ExternalInput").ap()
    cvp = dt("cvp", [128, 32 + 16 * 3], F32, kind="ExternalInput").ap()
    dww = dt("dww", [128, NCH, DWK], F32, kind="ExternalInput").ap()
    halo_mask = dt("halo_mask", [128, 1], F32, kind="ExternalInput").ap()
    ident_f_d = dt("ident_f", [128, 128], F32, kind="ExternalInput").ap()
    masks_d = dt("masks", [128, 2, 128], F32, kind="ExternalInput").ap()
    sel_d = dt("sel", [128, 4, 128], F32, kind="ExternalInput").ap()
    oaT = dt("oaT", [NH * HD, NTOK], BF16, kind="ExternalOutput").ap()
    cT = dt("cT", [NCH * 128, NCV], BF16, kind="ExternalOutput").ap()
    outs = []

    ident_f = P.sbuf("ident_f", [128, 128], F32)
    P.dma("sp", ident_f, T(ident_f_d))
    ident_b = P.sbuf("ident_b", [128, 128], BF16)
    P.op("dve", "tensor_copy", A(ident_b.ap, ident_f.ap), reads=[ident_f], writes=[ident_b])
    ones_b = P.sbuf("ones_b", [128, 128], BF16)
    P.op("dve", "memset", A(ones_b.ap, 1.0), writes=[ones_b])
    NW = int(os.environ.get("MK_NW", 4))
    wbig = P.sbuf("wbuf", [128, NW, KC * 128], BF16)
    wb = [T(wbig.ap[:, i, :], f"wb{i}") for i in range(NW)]
    wi = [0]

    def load_w(W, m0, mw=128):
        buf = wb[wi[0] % NW]
        wi[0] += 1
        dst = V(buf, buf.ap[:, 0:KC * mw].rearrange("p (k m) -> p k m", m=mw))
        src = T(W[:, m0:m0 + mw].rearrange("(k p) m -> p k m", p=128))
        P.dma("pool", dst, src)
        return dst

    bankF = [P.psum(f"bankF{i}", [128, 512], F32) for i in range(6)]
    bankB = [P.psum(f"bankB{i}", [128, 1024], BF16) for i in range(2)]

    def recip_sqrt(t_ap, tile):
        P.op("act", "activation", A(t_ap, t_ap, AF.Sqrt), reads=[tile], writes=[tile])
        P.op("dve", "reciprocal", A(t_ap, t_ap), reads=[tile], writes=[tile])

    if do_cv:
        NTC = 32 + NCV
        NBC = NCV // 512
        cvps = P.sbuf("cvp", [128, 80], F32)
        P.dma("sp", cvps, T(cvp))
        dwws = P.sbuf("dww", [128, NCH, DWK], F32)
        P.dma("sp", dwws, T(dww))
        hmask = P.sbuf("hmask", [128, 1], F32)
        P.dma("sp", hmask, T(halo_mask))
        mcin = P.mark()
        cin = [P.sbuf(f"cin{i}", [128, NTC], BF16) for i in range(NCH)]
        m0 = P.mark()
        hcv_big = P.sbuf("hcv", [128, KC, NTC], BF16)
        hcv = [T(hcv_big.ap[:, k, :], f"hcv{k}") for k in range(KC)]
        sgt = [P.sbuf(f"sgt{i}", [128, NTC], F32) for i in range(2)]
        for k in range(KC):
            P.dma("sp", hcv[k], T(hT_cv[k * 128:(k + 1) * 128, :]))
        pieces = [(0, 512), (512, 1024), (1024, NTC)]
        for p in range(NCH):
            wg = load_w(w_cv, 2048 + p * 128)
            for bi, (a, b) in enumerate(pieces):
                for k in range(KC):
                    P.op("pe", "matmul", A(bankF[bi].ap[:, 0:b - a], wg.ap[:, k, :], hcv[k].ap[:, a:b], start=(k == 0), stop=(k == KC - 1)),
                         reads=[wg, hcv[k]], writes=[bankF[bi]])
            wv = load_w(w_cv, p * 128)
            for bi, (a, b) in enumerate(pieces):
                for k in range(KC):
                    P.op("pe", "matmul", A(bankF[3 + bi].ap[:, 0:b - a], wv.ap[:, k, :], hcv[k].ap[:, a:b], start=(k == 0), stop=(k == KC - 1)),
                         reads=[wv, hcv[k]], writes=[bankF[3 + bi]])
            sg = sgt[p % 2]
            for bi, (a, b) in enumerate(pieces):
                P.op("act", "activation", A(sg.ap[:, a:b], bankF[bi].ap[:, 0:b - a], AF.Sigmoid, bias=cvps.ap[:, 16 + p:17 + p]), reads=[bankF[bi], cvps], writes=[sg])
            for bi, (a, b) in enumerate(pieces):
                P.op("dve", "scalar_tensor_tensor", A(cin[p].ap[:, a:b], bankF[3 + bi].ap[:, 0:b - a], cvps.ap[:, p:p + 1], sg.ap[:, a:b], ALU.add, ALU.mult),
                     reads=[bankF[3 + bi], cvps, sg], writes=[cin[p]])
            P.op("dve", "tensor_scalar", A(cin[p].ap[:, 0:32], cin[p].ap[:, 0:32], hmask.ap[:, 0:1], None, ALU.mult), reads=[cin[p], hmask], writes=[cin[p]])
        P.barrier()
        P.release(m0)
        cconv = [P.sbuf(f"cconv{i}", [128, NCV], F32) for i in range(NCH)]
        ddg = [P.sbuf(f"ddg{i}", [128, DWK, 128], BF16) for i in range(2)]
        xb = [P.sbuf(f"xb{i}", [128, 512], BF16) for i in range(2)]
        sq2 = [P.sbuf(f"sq2{i}", [128, 512], BF16) for i in range(2)]
        mean = P.sbuf("mean", [128, NCV], F32)
        rstd = P.sbuf("rstdc", [128, NCV], F32)
        cst = [P.sbuf(f"cst{i}", [128, NCV], BF16) for i in range(2)]
        tmpc = [P.sbuf(f"tmpc{i}", [128, NCV], F32) for i in range(2)]
        cnt = 0
        for ch in range(NCH):
            dd = ddg[ch % 2]
            for k in range(DWK):
                P.op("dve", "tensor_scalar", A(dd.ap[:, k, :], ident_f.ap, dwws.ap[:, ch, k:k + 1], None, ALU.mult), reads=[ident_f, dwws], writes=[dd])
            for nb in range(NBC):
                ps = bankF[nb % 2]
                for k in range(DWK):
                    P.op("pe", "matmul", A(ps.ap, dd.ap[:, k, :], cin[ch].ap[:, 2 + k + nb * 512:2 + k + (nb + 1) * 512], start=(k == 0), stop=(k == DWK - 1)),
                         reads=[dd, cin[ch]], writes=[ps])
                sl = slice(nb * 512, (nb + 1) * 512)
                P.op("act", "activation", A(cconv[ch].ap[:, sl], ps.ap, AF.Identity, bias=cvps.ap[:, 32 + ch:33 + ch]), reads=[ps, cvps], writes=[cconv[ch]])
                x_, s_ = xb[cnt % 2], sq2[cnt % 2]
                cnt += 1
                P.op("act", "copy", A(x_.ap, cconv[ch].ap[:, sl]), reads=[cconv[ch]], writes=[x_])
                P.op("act", "activation", A(s_.ap, cconv[ch].ap[:, sl], AF.Square), reads=[cconv[ch]], writes=[s_])
                P.op("pe", "matmul", A(bankF[2 + nb].ap, ones_b.ap, x_.ap, start=(ch == 0), stop=(ch == NCH - 1)), reads=[ones_b, x_], writes=[bankF[2 + nb]])
                P.op("pe", "matmul", A(bankF[4 + nb].ap, ones_b.ap, s_.ap, start=(ch == 0), stop=(ch == NCH - 1)), reads=[ones_b, s_], writes=[bankF[4 + nb]])
        for nb in range(NBC):
            sl = slice(nb * 512, (nb + 1) * 512)
            P.op("dve", "tensor_scalar", A(mean.ap[:, sl], bankF[2 + nb].ap, 1.0 / (NCH * 128), None, ALU.mult), reads=[bankF[2 + nb]], writes=[mean])
            P.op("dve", "tensor_scalar", A(rstd.ap[:, sl], bankF[4 + nb].ap, 1.0 / (NCH * 128), LN_EPS, ALU.mult, ALU.add), reads=[bankF[4 + nb]], writes=[rstd])
        P.op("dve", "tensor_tensor", A(tmpc[0].ap, mean.ap, mean.ap, ALU.mult), reads=[mean], writes=[tmpc[0]])
        P.op("dve", "tensor_tensor", A(rstd.ap, rstd.ap, tmpc[0].ap, ALU.subtract), reads=[rstd, tmpc[0]], writes=[rstd])
        recip_sqrt(rstd.ap, rstd)
        for ch in range(NCH):
            tt = tmpc[ch % 2]
            P.op("dve", "tensor_tensor", A(tt.ap, cconv[ch].ap, mean.ap, ALU.subtract), reads=[cconv[ch], mean], writes=[tt])
            P.op("dve", "tensor_tensor", A(tt.ap, tt.ap, rstd.ap, ALU.mult), reads=[tt, rstd], writes=[tt])
            P.op("act", "activation", A(cst[ch % 2].ap, tt.ap, AF.Silu, bias=cvps.ap[:, 64 + ch:65 + ch], scale=cvps.ap[:, 48 + ch:49 + ch]), reads=[tt, cvps], writes=[cst[ch % 2]])
            o = T(cT[ch * 128:(ch + 1) * 128, :])
            outs.append(o)
            P.dma("sp", o, cst[ch % 2])
        P.barrier()
        P.release(mcin)


    if do_gdn:
        NB = TBK // 512
        NCK = TBK // C
        masks = P.sbuf("masks", [128, 2, 128], F32)
        P.dma("sp", masks, T(masks_d))
        sel = P.sbuf("sel", [128, 4, 128], F32)
        P.dma("sp", sel, T(sel_d))
        cw = P.sbuf("convw", [128, 12, 4], F32)
        P.dma("sp", cw, T(convw))
        hps = P.sbuf("hp", [128, 2], F32)
        P.dma("sp", hps, T(hp))
        gn = P.sbuf("gnw", [128, 1], F32)
        P.dma("sp", gn, T(gnw))
        NR = 68
        nA = P.sbuf("nA", [128, 1], F32)
        P.op("act", "activation", A(nA.ap, hps.ap[:, 0:1], AF.Exp), reads=[hps], writes=[nA])
        P.op("dve", "tensor_scalar", A(nA.ap, nA.ap, -1.0, None, ALU.mult), reads=[nA], writes=[nA])
        ones4 = P.sbuf("ones4", [128, 128], F32)
        P.op("dve", "memset", A(ones4.ap, 1.0), writes=[ones4])
        wlp = P.sbuf("wlp", [128, KC, NR], BF16)
        P.op("pool", "memset", A(wlp.ap, 0.0), writes=[wlp])
        rn = P.sbuf("rn", [128, TBK], F32)
        dg = P.sbuf("dg", [128, 12 * 4, 128], BF16)
        for i in range(12):
            for k in range(4):
                P.op("dve", "tensor_scalar", A(dg.ap[:, i * 4 + k, :], ident_f.ap, cw.ap[:, i, k:k + 1], None, ALU.mult),
                     reads=[ident_f, cw], writes=[dg])
        hb_big = P.sbuf("hb", [128, KC, TBK], BF16)
        hb = [T(hb_big.ap[:, k, :], f"hb{k}") for k in range(KC)]
        pjb = [P.sbuf(f"pjb{i}", [128, 4 + TBK], BF16) for i in range(12)]
        for i in range(12):
            P.op("pool", "memset", A(pjb[i].ap[:, 0:4], 0.0), writes=[pjb[i]])
        sz = [P.sbuf(f"sz{h}", [128, TBK], F32) for h in range(NH)]
        QT = [P.sbuf(f"QT{h}", [128, TBK], BF16) for h in range(NH)]
        KT = [P.sbuf(f"KT{h}", [128, TBK], BF16) for h in range(NH)]
        KTb = [P.sbuf(f"KTb{h}", [128, TBK], BF16) for h in range(NH)]
        QG = [P.sbuf(f"QG{h}", [128, TBK], BF16) for h in range(NH)]
        VT = [P.sbuf(f"VT{h}", [128, TBK], BF16) for h in range(NH)]
        gcr = [P.sbuf(f"gcr{h}", [128, TBK], F32) for h in range(NH)]
        egl = [P.sbuf(f"egl{h}", [128, NCK], F32) for h in range(NH)]
        scr = [P.sbuf(f"scr{i}", [128, TBK], F32) for i in range(3)]
        sqb = [P.sbuf(f"sqb{i}", [128, TBK], BF16) for i in range(2)]
        lg = P.sbuf("lg", [128, TBK], F32)
        gg = P.sbuf("gg", [128, TBK], F32)
        gc4 = P.sbuf("gc4", [128, TBK], F32)
        lt = scr
        rows = P.sbuf("rows", [128, TBK], F32)
        P.op("pool", "memset", A(rows.ap, 0.0), writes=[rows])
        cols = [P.sbuf(f"cols{c}", [128, 128], F32) for c in range(NCK)]
        ebg = [P.sbuf(f"ebg{c}", [128, 4], F32) for c in range(NCK)]
        S_f = [P.sbuf(f"Sf{h}", [128, 128], F32) for h in range(NH)]
        S_b = [P.sbuf(f"Sb{h}", [128, 128], BF16) for h in range(NH)]
        for h in range(NH):
            P.op("pool", "memset", A(S_f[h].ap, 0.0), writes=[S_f[h]])
            P.op("pool", "memset", A(S_b[h].ap, 0.0), writes=[S_b[h]])
        CW = int(os.environ.get("MK_CW", 2))
        NS = CW * NH
        def mk(name, dtype):
            return [P.sbuf(f"{name}{i}", [128, 128], dtype) for i in range(NS)]
        kbg, ktl, vb, dmt, dms, Bm, BmT, aT, Pm, uS, wT, vn, on = (
            mk("kbg", BF16), mk("ktl", BF16), mk("vb", BF16), mk("dmt", F32), mk("dms", F32),
            mk("Bm", BF16), mk("BmT", BF16), mk("aT", BF16), mk("Pm", BF16), mk("uS", F32), mk("wT", BF16),
            mk("vn", BF16), mk("on", BF16))
        Mk = [mk(f"Mk{k}_", BF16) for k in range(2)]
        MkT = [mk(f"MkT{k}_", BF16) for k in range(2)]
        ssum = [P.sbuf(f"ssum{i}", [128, 1], F32) for i in range(NS)]
        ost = [P.sbuf(f"ost{i}", [128, 128], BF16) for i in range(NS)]
        psA = bankF[0:2]
        psS = [V(t, t.ap[:, 0:128]) for t in bankF[2:6]] + [V(t, t.ap[:, 0:128]) for t in bankF[0:2]]
        psT = [V(t, t.ap[:, 0:128]) for t in bankB]
        pa = [0]
        pss = [0]
        pst = [0]

        def nextT():
            t = psT[pst[0] % len(psT)]
            pst[0] += 1
            return t

        def pt_b(t):
            return t.ap

        def nextA():
            t = psA[pa[0] % len(psA)]
            pa[0] += 1
            return t

        def nextS():
            t = psS[pss[0] % len(psS)]
            pss[0] += 1
            return t

        def mm(ps_view, lhsT, rhs, start=True, stop=True, reads=(), ps_tile=None):
            P.op("pe", "matmul", A(_ap(ps_view), _ap(lhsT), _ap(rhs), start=start, stop=stop),
                 reads=[lhsT, rhs] + list(reads), writes=[ps_tile if ps_tile is not None else ps_view])

        for tb in range(NTOK // TBK):
            t0 = tb * TBK
            if tb == 0:
                for k in range(KC):
                    P.dma("sp", hb[k], T(hT[k * 128:(k + 1) * 128, t0:t0 + TBK]))
            wl = load_w(w_gdn, 16 * 128, 8)
            P.op("dve", "tensor_copy", A(wlp.ap[:, :, 0:4], wl.ap[:, :, 4:8]), reads=[wl], writes=[wlp])
            P.op("dve", "tensor_copy", A(wlp.ap[:, :, 32:36], wl.ap[:, :, 0:4]), reads=[wl], writes=[wlp])
            P.op("dve", "tensor_copy", A(wlp.ap[:, :, 64:68], wl.ap[:, :, 4:8]), reads=[wl], writes=[wlp])
            for nb in range(NB):
                sl = slice(nb * 512, (nb + 1) * 512)
                psb = nextA()
                for k in range(KC):
                    P.op("pe", "matmul", A(psb.ap[0:NR, :], wlp.ap[:, k, :], hb[k].ap[:, nb * 512:(nb + 1) * 512], start=(k == 0), stop=(k == KC - 1)),
                         reads=[wlp, hb[k]], writes=[psb])
                P.op("act", "copy", A(lg.ap[0:NR, sl], psb.ap[0:NR, :]), reads=[psb], writes=[lg])
            R = slice(0, NR)
            P.op("dve", "tensor_scalar", A(lt[0].ap[R, :], lg.ap[R, :], hps.ap[R, 1:2], None, ALU.add), reads=[lg, hps], writes=[lt[0]])
            P.op("dve", "tensor_scalar", A(lt[1].ap[R, :], lt[0].ap[R, :], -1.0, None, ALU.mult), reads=[lt[0]], writes=[lt[1]])
            P.op("dve", "tensor_tensor", A(lt[1].ap[R, :], lt[1].ap[R, :], lt[0].ap[R, :], ALU.min), reads=[lt[0], lt[1]], writes=[lt[1]])
            P.op("act", "activation", A(lt[1].ap[R, :], lt[1].ap[R, :], AF.Exp), reads=[lt[1]], writes=[lt[1]])
            P.op("act", "activation", A(lt[1].ap[R, :], lt[1].ap[R, :], AF.Ln, bias=1.0), reads=[lt[1]], writes=[lt[1]])
            P.op("dve", "tensor_scalar", A(lt[2].ap[R, :], lt[0].ap[R, :], 0.0, None, ALU.max), reads=[lt[0]], writes=[lt[2]])
            P.op("dve", "tensor_tensor", A(lt[2].ap[R, :], lt[2].ap[R, :], lt[1].ap[R, :], ALU.add), reads=[lt[2], lt[1]], writes=[lt[2]])
            P.op("dve", "tensor_scalar", A(gg.ap[R, :], lt[2].ap[R, :], nA.ap[R, 0:1], None, ALU.mult), reads=[lt[2], nA], writes=[gg])
            for c in range(NCK):
                cs = slice(c * C, (c + 1) * C)
                P.op("dve", "tensor_tensor_scan", A(gc4.ap[R, cs], ones4.ap[R, :], gg.ap[R, cs], 0.0, ALU.mult, ALU.add), reads=[ones4, gg], writes=[gc4])
            P.op("dve", "tensor_copy", A(rows.ap[0:4, :], gc4.ap[0:4, :]), reads=[gc4], writes=[rows])
            P.op("act", "activation", A(rows.ap[32:36, :], lg.ap[32:36, :], AF.Sigmoid), reads=[lg], writes=[rows])
            for c in range(NCK):
                cs = slice(c * C, (c + 1) * C)
                P.op("act", "activation", A(rows.ap[64:68, cs], gc4.ap[64:68, cs], AF.Exp, bias=gc4.ap[64:68, c * C + C - 1:c * C + C], scale=-1.0),
                     reads=[gc4], writes=[rows])
            if os.environ.get("MK_P2STOP") == "logits":
                o = T(oaT[0:128, t0:t0 + 128]); outs.append(o)
                P.op("dve", "tensor_copy", A(ost[0].ap, cols[0].ap), reads=[cols[0]], writes=[ost[0]])
                P.dma("sp", o, ost[0])
                continue
            def stageA(h):
                for ty in range(4):
                    wv = load_w(w_gdn, (h * 4 + ty) * 128)
                    for nb in range(NB):
                        ps = nextA()
                        for k in range(KC):
                            P.op("pe", "matmul", A(ps.ap, wv.ap[:, k, :], hb[k].ap[:, nb * 512:(nb + 1) * 512], start=(k == 0), stop=(k == KC - 1)),
                                 reads=[wv, hb[k]], writes=[ps])
                        sl = slice(nb * 512, (nb + 1) * 512)
                        if ty < 3:
                            pj = pjb[h * 3 + ty]
                            P.op("act", "copy", A(pj.ap[:, 4 + nb * 512:4 + (nb + 1) * 512], ps.ap), reads=[ps], writes=[pj])
                        else:
                            P.op("act", "activation", A(sz[h].ap[:, sl], ps.ap, AF.Silu), reads=[ps], writes=[sz[h]])
            def stageB(h):
                for nb in range(NB):
                    sl = slice(nb * 512, (nb + 1) * 512)
                    ps = nextA()
                    P.op("pe", "matmul", A(ps.ap, sel.ap[0:4, h, :], rows.ap[0:4, sl], start=True, stop=True), reads=[sel, rows], writes=[ps])
                    P.op("act", "copy", A(gcr[h].ap[:, sl], ps.ap), reads=[ps], writes=[gcr[h]])
                    ps2 = nextA()
                    P.op("pe", "matmul", A(ps2.ap, sel.ap[32:36, h, :], rows.ap[32:36, sl], start=True, stop=True), reads=[sel, rows], writes=[ps2])
                    P.op("act", "copy", A(scr[2].ap[:, sl], ps2.ap), reads=[ps2], writes=[scr[2]])
                for c in range(NCK):
                    P.op("act", "activation", A(egl[h].ap[:, c:c + 1], gcr[h].ap[:, c * C + C - 1:c * C + C], AF.Exp), reads=[gcr[h]], writes=[egl[h]])
                for ty in range(3):
                    pj = pjb[h * 3 + ty]
                    i = h * 3 + ty
                    for nb in range(NB):
                        ps = nextA()
                        for k in range(4):
                            P.op("pe", "matmul", A(ps.ap, dg.ap[:, i * 4 + k, :], pj.ap[:, 1 + k + nb * 512:1 + k + (nb + 1) * 512], start=(k == 0), stop=(k == 3)),
                                 reads=[dg, pj], writes=[ps])
                        sl = slice(nb * 512, (nb + 1) * 512)
                        if ty == 2:
                            P.op("act", "activation", A(VT[h].ap[:, sl], ps.ap, AF.Silu), reads=[ps], writes=[VT[h]])
                        else:
                            P.op("act", "activation", A(scr[ty].ap[:, sl], ps.ap, AF.Silu), reads=[ps], writes=[scr[ty]])
                    P.op("dve", "tensor_copy", A(pj.ap[:, 1:4], pj.ap[:, TBK + 1:TBK + 4]), reads=[pj], writes=[pj])
                for ty in range(2):
                    src = scr[ty]
                    for nb in range(NB):
                        sl = slice(nb * 512, (nb + 1) * 512)
                        sq = sqb[nb % 2]
                        P.op("act", "activation", A(sq.ap[:, 0:512], src.ap[:, sl], AF.Square), reads=[src], writes=[sq])
                        ps = nextA()
                        P.op("pe", "matmul", A(ps.ap, ones_b.ap, sq.ap[:, 0:512], start=True, stop=True), reads=[ones_b, sq], writes=[ps])
                        dstn = QT[h] if ty == 0 else KT[h]
                        P.op("dve", "tensor_scalar", A(rn.ap[:, sl], ps.ap, 1e-6, None, ALU.add), reads=[ps], writes=[rn])
                    recip_sqrt(rn.ap, rn)
                    scale = (HD ** -0.5) if ty == 0 else 1.0
                    P.op("dve", "scalar_tensor_tensor", A(dstn.ap, src.ap, scale, rn.ap, ALU.mult, ALU.mult), reads=[src, rn], writes=[dstn])
                P.op("dve", "tensor_tensor", A(KTb[h].ap, KT[h].ap, scr[2].ap, ALU.mult), reads=[KT[h], scr[2]], writes=[KTb[h]])
                P.op("act", "activation", A(rn.ap, gcr[h].ap, AF.Exp), reads=[gcr[h]], writes=[rn])
                P.op("dve", "tensor_tensor", A(QG[h].ap, QT[h].ap, rn.ap, ALU.mult), reads=[QT[h], rn], writes=[QG[h]])
            stageA(0)
            for h in range(NH):
                if h + 1 < NH:
                    stageA(h + 1)
                stageB(h)
            for c in range(NCK):
                cs = slice(c * C, (c + 1) * C)
                pt = nextS()
                P.op("pe", "transpose", A(pt.ap, rows.ap[:, cs], ident_f.ap), reads=[rows, ident_f], writes=[pt])
                P.op("act", "copy", A(cols[c].ap, pt.ap), reads=[pt], writes=[cols[c]])
                P.op("act", "activation", A(ebg[c].ap, cols[c].ap[:, 0:4], AF.Exp), reads=[cols[c]], writes=[ebg[c]])
                P.op("dve", "tensor_tensor", A(ebg[c].ap, ebg[c].ap, cols[c].ap[:, 32:36], ALU.mult), reads=[ebg[c], cols[c]], writes=[ebg[c]])
            if tb + 1 < NTOK // TBK:
                for k in range(KC):
                    P.dma("sp", hb[k], T(hT[k * 128:(k + 1) * 128, t0 + TBK:t0 + 2 * TBK]))
            if os.environ.get("MK_P2STOP") == "proj":
                for h in range(NH):
                    o = T(oaT[h * 128:(h + 1) * 128, t0:t0 + TBK]); outs.append(o)
                    P.dma("sp", o, QG[h])
                continue
            for w0 in range(0, NCK, CW):
                probs = [(w0 + ci, h, ci * NH + h) for ci in range(CW) for h in range(NH)]

                def vw(t, c):
                    return V(t, t.ap[:, c * C:(c + 1) * C])
                for (c, h, s) in probs:
                    pt = nextT()
                    P.op("pe", "transpose", A(pt.ap, vw(KT[h], c).ap, ident_b.ap), reads=[KT[h], ident_b], writes=[pt])
                    P.op("act", "activation", A(kbg[s].ap, pt.ap, AF.Copy, scale=ebg[c].ap[:, h:h + 1]), reads=[pt, ebg[c]], writes=[kbg[s]])
                    P.op("dve", "tensor_scalar", A(ktl[s].ap, pt.ap, cols[c].ap[:, 64 + h:65 + h], None, ALU.mult), reads=[pt, cols[c]], writes=[ktl[s]])
                for (c, h, s) in probs:
                    pt2 = nextT()
                    P.op("pe", "transpose", A(pt2.ap, vw(VT[h], c).ap, ident_b.ap), reads=[VT[h], ident_b], writes=[pt2])
                    P.op("act", "activation", A(vb[s].ap, pt2.ap, AF.Copy, scale=cols[c].ap[:, 32 + h:33 + h]), reads=[pt2, cols[c]], writes=[vb[s]])
                for (c, h, s) in probs:
                    P.op("dve", "tensor_scalar", A(dmt[s].ap, gcr[h].ap[:, c * C:(c + 1) * C], cols[c].ap[:, h:h + 1], 0.0, ALU.subtract, ALU.min), reads=[gcr[h], cols[c]], writes=[dmt[s]])
                for (c, h, s) in probs:
                    P.op("act", "activation", A(dmt[s].ap, dmt[s].ap, AF.Exp), reads=[dmt[s]], writes=[dmt[s]])
                for (c, h, s) in probs:
                    P.op("dve", "tensor_tensor", A(dms[s].ap, dmt[s].ap, masks.ap[:, 0, :], ALU.mult), reads=[dmt[s], masks], writes=[dms[s]])
                    P.op("dve", "tensor_tensor", A(dmt[s].ap, dmt[s].ap, masks.ap[:, 1, :], ALU.mult), reads=[dmt[s], masks], writes=[dmt[s]])
                for (c, h, s) in probs:
                    p1 = nextS()
                    mm(p1.ap, vw(KT[h], c), vw(KTb[h], c), ps_tile=p1)
                    P.op("dve", "tensor_tensor", A(Bm[s].ap, p1.ap, dms[s].ap, ALU.mult), reads=[p1, dms[s]], writes=[Bm[s]])
                for (c, h, s) in probs:
                    p2 = nextS()
                    mm(p2.ap, vw(KT[h], c), vw(QT[h], c), ps_tile=p2)
                    P.op("dve", "tensor_tensor", A(aT[s].ap, p2.ap, dmt[s].ap, ALU.mult), reads=[p2, dmt[s]], writes=[aT[s]])
                for (c, h, s) in probs:
                    p3 = nextT()
                    P.op("pe", "transpose", A(p3.ap, Bm[s].ap, ident_b.ap), reads=[Bm[s], ident_b], writes=[p3])
                    P.op("act", "copy", A(BmT[s].ap, p3.ap), reads=[p3], writes=[BmT[s]])
                    P.op("dve", "tensor_tensor", A(Pm[s].ap, Bm[s].ap, ident_f.ap, ALU.add), reads=[Bm[s], ident_f], writes=[Pm[s]])
                cur = {s: (Bm[s], BmT[s]) for (_, _, s) in probs}
                for it in range(6):
                    for (c, h, s) in probs:
                        M, MT = cur[s]
                        pq = nextS()
                        mm(pq.ap, M, MT, ps_tile=pq)
                        P.op("act", "copy", A(MkT[it % 2][s].ap, pq.ap), reads=[pq], writes=[MkT[it % 2][s]])
                    if it < 5:
                        for (c, h, s) in probs:
                            M, MT = cur[s]
                            pr = nextS()
                            mm(pr.ap, MT, M, ps_tile=pr)
                            P.op("dve", "tensor_copy", A(Mk[it % 2][s].ap, pr.ap), reads=[pr], writes=[Mk[it % 2][s]])
                    for (c, h, s) in probs:
                        pp = nextS()
                        mm(pp.ap, ident_b, Pm[s], start=True, stop=False, ps_tile=pp)
                        mm(pp.ap, MkT[it % 2][s], Pm[s], start=False, stop=True, ps_tile=pp)
                        P.op("dve" if s % 2 == 0 else "act", "tensor_copy" if s % 2 == 0 else "copy", A(Pm[s].ap, pp.ap), reads=[pp], writes=[Pm[s]])
                    for (c, h, s) in probs:
                        cur[s] = (Mk[it % 2][s], MkT[it % 2][s])
                for (c, h, s) in probs:
                    pu = nextS()
                    mm(pu.ap, Pm[s], vb[s], ps_tile=pu)
                    P.op("act", "copy", A(uS[s].ap, pu.ap), reads=[pu], writes=[uS[s]])
                for (c, h, s) in probs:
                    pw = nextS()
                    mm(pw.ap, kbg[s], Pm[s], ps_tile=pw)
                    P.op("dve", "tensor_copy", A(wT[s].ap, pw.ap), reads=[pw], writes=[wT[s]])
                for ci in range(CW):
                    pl = [(c, h, s) for (c, h, s) in probs if c == w0 + ci]
                    for (c, h, s) in pl:
                        pv = nextS()
                        mm(pv.ap, wT[s], S_b[h], ps_tile=pv)
                        P.op("dve", "tensor_tensor", A(vn[s].ap, uS[s].ap, pv.ap, ALU.subtract), reads=[uS[s], pv], writes=[vn[s]])
                    for (c, h, s) in pl:
                        po = nextS()
                        mm(po.ap, vw(QG[h], c), S_b[h], start=True, stop=False, ps_tile=po)
                        mm(po.ap, aT[s], vn[s], start=False, stop=True, ps_tile=po)
                        P.op("act", "copy", A(uS[s].ap, po.ap), reads=[po], writes=[uS[s]])
                    for (c, h, s) in pl:
                        pS = nextS()
                        mm(pS.ap, ktl[s], vn[s], ps_tile=pS)
                        P.op("dve", "scalar_tensor_tensor", A(S_f[h].ap, S_f[h].ap, egl[h].ap[:, c:c + 1], pS.ap, ALU.mult, ALU.add),
                             reads=[S_f[h], egl[h], pS], writes=[S_f[h]])
                        P.op("act", "copy", A(S_b[h].ap, S_f[h].ap), reads=[S_f[h]], writes=[S_b[h]])
                for (c, h, s) in probs:
                    P.op("act", "activation", A(dms[s].ap, uS[s].ap, AF.Square, accum_out=ssum[s].ap), reads=[uS[s]], writes=[dms[s], ssum[s]])
                for (c, h, s) in probs:
                    P.op("dve", "tensor_scalar", A(ssum[s].ap, ssum[s].ap, 1.0 / HD, RMS_EPS, ALU.mult, ALU.add), reads=[ssum[s]], writes=[ssum[s]])
                for (c, h, s) in probs:
                    P.op("act", "activation", A(ssum[s].ap, ssum[s].ap, AF.Sqrt), reads=[ssum[s]], writes=[ssum[s]])
                for (c, h, s) in probs:
                    P.op("dve", "reciprocal", A(ssum[s].ap, ssum[s].ap), reads=[ssum[s]], writes=[ssum[s]])
                for (c, h, s) in probs:
                    P.op("act", "activation", A(on[s].ap, uS[s].ap, AF.Copy, scale=ssum[s].ap[:, 0:1]), reads=[uS[s], ssum[s]], writes=[on[s]])
                for (c, h, s) in probs:
                    pz = nextT()
                    P.op("pe", "transpose", A(pz.ap, on[s].ap, ident_b.ap), reads=[on[s], ident_b], writes=[pz])
                    P.op("dve", "scalar_tensor_tensor", A(ost[s].ap, pz.ap, gn.ap[:, 0:1], sz[h].ap[:, c * C:(c + 1) * C], ALU.mult, ALU.mult),
                         reads=[pz, gn, sz[h]], writes=[ost[s]])
                    o = T(oaT[h * HD:(h + 1) * HD, t0 + c * C:t0 + (c + 1) * C])
                    outs.append(o)
                    P.dma("sp", o, ost[s])
    P.wait_tile("sp", outs)
    P.emit()
    P.close()
    return P


import ml_dtypes
from concourse.bass_utils import run_bass_kernel_spmd

_BF = ml_dtypes.bfloat16
_PROGS = {}
NTC_CORE = 1024
SEQ = 4096
NCORES = 8


def _prog(name, builder):
    if name not in _PROGS:
        nc = bass.Bass("TRN2", target_bir_lowering=False)
        builder(nc)
        _PROGS[name] = nc
    return _PROGS[name]


def _nl(vs):
    return np.ascontiguousarray(np.stack([np.asarray(v, np.float32).reshape(KC, 128).T for v in vs], axis=1))


def _consts():
    ident = np.eye(128, dtype=np.float32)
    jj, ii = np.meshgrid(np.arange(128), np.arange(128), indexing="ij")
    masks = np.ascontiguousarray(np.stack([-(ii > jj).astype(np.float32), (ii >= jj).astype(np.float32)], axis=1))
    sel = np.zeros((128, 4, 128), np.float32)
    for hh in range(4):
        sel[hh, hh, :] = 1.0
        sel[32 + hh, hh, :] = 1.0
    return ident, masks, sel


def _p2_inputs(l, b, r, hT_full, hT_own, inp, consts):
    w_in = inp["w_in"][l]
    cols = []
    for hl in range(4):
        head = 4 * r + hl
        for ty in range(3):
            cols.append(np.arange(ty * 2048 + head * 128, ty * 2048 + (head + 1) * 128))
        cols.append(np.arange(6144 + head * 128, 6144 + (head + 1) * 128))
    cols.append(np.arange(8192 + 4 * r, 8192 + 4 * r + 4))
    cols.append(np.arange(8208 + 4 * r, 8208 + 4 * r + 4))
    w_gdn = np.ascontiguousarray(w_in[:, np.concatenate(cols)])
    gcw = inp["gdn_conv_w"][l]
    convw = np.zeros((128, 12, 4), np.float32)
    for hl in range(4):
        head = 4 * r + hl
        for ty in range(3):
            convw[:, hl * 3 + ty, :] = gcw[:, ty * 2048 + head * 128: ty * 2048 + (head + 1) * 128].T
    hp = np.zeros((128, 2), np.float32)
    for base in (0, 64):
        hp[base:base + 4, 0] = inp["gdn_a_log"][l][4 * r:4 * r + 4]
        hp[base:base + 4, 1] = inp["gdn_dt_bias"][l][4 * r:4 * r + 4]
    cvp = np.ascontiguousarray(np.concatenate([
        inp["cm_pw_b"][l].reshape(32, 128).T, inp["cm_dw_b"][l].reshape(16, 128).T,
        inp["cm_ln_w"][l].reshape(16, 128).T, inp["cm_ln_b"][l].reshape(16, 128).T], axis=1).astype(np.float32))
    dww = np.ascontiguousarray(inp["cm_dw_w"][l].reshape(31, 16, 128).transpose(2, 1, 0))
    if r == 0:
        halo = np.zeros((D, 32), _BF)
    else:
        halo = hT_full[:, NTC_CORE * r - 32:NTC_CORE * r]
    hT_cv = np.ascontiguousarray(np.concatenate([halo, hT_own], axis=1))
    ident, masks, sel = consts
    return {"hT": hT_full, "hT_cv": hT_cv, "w_gdn": w_gdn, "convw": convw, "hp": hp,
            "gnw": np.ascontiguousarray(inp["gdn_norm_w"][l].reshape(128, 1).astype(np.float32)),
            "cvp": cvp, "dww": dww, "halo_mask": np.full((128, 1), 0.0 if r == 0 else 1.0, np.float32),
            "ident_f": ident, "masks": masks, "sel": sel}


def kernel(**inputs):
    inp = {k: np.asarray(v) for k, v in inputs.items()}
    x = inp["x"].astype(np.float32, copy=False)
    cores = [(b, r) for b in range(2) for r in range(4)]
    ids = list(range(NCORES))
    consts = _consts()
    xT = [np.ascontiguousarray(x[b, r * NTC_CORE:(r + 1) * NTC_CORE, :].T) for (b, r) in cores]
    nc1 = _prog("p1", lambda nc: build_p1(nc, NTC_CORE))
    nc2 = _prog("p2", lambda nc: build_p2(nc, SEQ, NCV=NTC_CORE))
    nc3 = _prog("p3", lambda nc: build_p3(nc, NTC_CORE))
    n0 = _nl([inp["pre_mix_norm"][0]])
    res = run_bass_kernel_spmd(nc1, [{"xT": xT[c], "norms": n0} for c in ids], core_ids=ids)
    hT = [res.results[c]["h_out"] for c in ids]
    depth = inp["w_in"].shape[0]
    for l in range(depth):
        hT_full = [np.ascontiguousarray(np.concatenate([hT[b * 4 + r] for r in range(4)], axis=1)) for b in range(2)]
        w_cv = np.ascontiguousarray(inp["w_in"][l][:, 8224:])
        maps = []
        for c, (b, r) in enumerate(cores):
            m = _p2_inputs(l, b, r, hT_full[b], hT[c], inp, consts)
            m["w_cv"] = w_cv
            maps.append(m)
        res = run_bass_kernel_spmd(nc2, maps, core_ids=ids)
        oaT = [res.results[c]["oaT"] for c in ids]
        cT = [res.results[c]["cT"] for c in ids]
        del maps
        nxt = inp["pre_mix_norm"][l + 1] if l + 1 < depth else np.ones(D, np.float32)
        nrm = _nl([inp["post_mix_norm"][l], inp["pre_ffn_norm"][l], inp["post_ffn_norm"][l], nxt])
        maps = []
        for c, (b, r) in enumerate(cores):
            mixT = np.ascontiguousarray(np.concatenate(
                [oaT[b * 4 + rr][:, r * NTC_CORE:(r + 1) * NTC_CORE] for rr in range(4)] + [cT[c]], axis=0))
            maps.append({"mixT": mixT, "xT": xT[c], "w_out": inp["w_out"][l], "w_gate": inp["w_gate"][l],
                         "w_up": inp["w_up"][l], "w_down": inp["w_down"][l], "norms": nrm})
        res = run_bass_kernel_spmd(nc3, maps, core_ids=ids)
        del maps
        xT = [res.results[c]["x_out"] for c in ids]
        hT = [res.results[c]["h_out"] for c in ids]
    out = np.empty_like(x)
    for c, (b, r) in enumerate(cores):
        out[b, r * NTC_CORE:(r + 1) * NTC_CORE, :] = xT[c].T
    return out
```
